# Optimizing a Trainium2 kernel written in Bass

```python
import jax, jax.numpy as jnp
from jax import lax
import numpy as np

D_MODEL = 1024
BATCH = 8
SEQ = 2048
DEPTH = 2

ATTN_W = D_MODEL // 2
N_ATTN_HEADS = 8
HEAD_DIM = ATTN_W // N_ATTN_HEADS
POOL_W = D_MODEL - ATTN_W
POOL_WINDOWS = (2, 4, 8, 16)
POOL_C = POOL_W // len(POOL_WINDOWS)
IN_COLS = 3 * ATTN_W + N_ATTN_HEADS + POOL_W
D_FF = 2816
CONV_W = 3
Q_BLOCK = 128
EPS = 1e-6

kernel_name = "hybrid_fox_pool_convffn_adaln"


def _rmsnorm(x, g):
    x32 = x.astype(jnp.float32)
    y = x32 * lax.rsqrt(jnp.mean(x32 * x32, axis=-1, keepdims=True) + EPS)
    return (y * g.astype(jnp.float32)).astype(x.dtype)


def _fox_attention(q, k, v, log_f):
    B, S, H, Dh = q.shape
    nb = S // Q_BLOCK
    F = jnp.cumsum(log_f, axis=1)
    Fk = F.transpose(0, 2, 1)
    q32 = q.astype(jnp.float32) * (Dh ** -0.5)
    k32 = k.astype(jnp.float32)
    v32 = v.astype(jnp.float32)
    qb = q32.reshape(B, nb, Q_BLOCK, H, Dh).transpose(1, 0, 2, 3, 4)
    Fq = F.reshape(B, nb, Q_BLOCK, H).transpose(1, 0, 3, 2)
    kpos = jnp.arange(S)

    def one_block(args):
        i, qi, Fi = args
        s = jnp.einsum('bqhd,bkhd->bhqk', qi, k32)
        s = s + Fi[..., None] - Fk[:, :, None, :]
        qpos = i * Q_BLOCK + jnp.arange(Q_BLOCK)
        s = jnp.where(qpos[:, None] >= kpos[None, :], s, -jnp.inf)
        p = jax.nn.softmax(s, axis=-1)
        return jnp.einsum('bhqk,bkhd->bqhd', p, v32)

    o = lax.map(one_block, (jnp.arange(nb), qb, Fq))
    return o.transpose(1, 0, 2, 3, 4).reshape(B, S, H * Dh)


def _multiscale_pool(u, pool_w, pool_scale):
    B, S, _ = u.shape
    u32 = u.astype(jnp.float32)
    count = jnp.arange(1, S + 1, dtype=jnp.float32)
    outs = []
    for g, w in enumerate(POOL_WINDOWS):
        ug = u32[..., g * POOL_C:(g + 1) * POOL_C]
        csp = jnp.concatenate([jnp.zeros((B, 1, POOL_C), jnp.float32),
                               jnp.cumsum(ug, axis=1)], axis=1)
        lo = jnp.concatenate([jnp.zeros((B, w - 1, POOL_C), jnp.float32),
                              csp[:, :S - w + 1]], axis=1)
        mean = (csp[:, 1:] - lo) / jnp.minimum(count, w)[None, :, None]
        outs.append(jnp.einsum('bsc,cd->bsd', mean - ug, pool_w[g].astype(jnp.float32)))
    out = jnp.concatenate(outs, axis=-1) * pool_scale.astype(jnp.float32)
    return out.astype(u.dtype)


def _conv_ffn(h, w_up, conv_w, conv_b, w_down):
    S = h.shape[1]
    a = h @ w_up
    ap = jnp.pad(a, ((0, 0), (CONV_W - 1, 0), (0, 0)))
    conv = conv_b + ap[:, 0:S] * conv_w[0]
    for j in range(1, CONV_W):
        conv = conv + ap[:, j:j + S] * conv_w[j]
    gate, val = jnp.split(conv, 2, axis=-1)
    return (jax.nn.silu(gate) * val) @ w_down


def setup_inputs(seed: int = 0) -> dict:
    key = jax.random.key(seed)
    ks = jax.random.split(key, 18)
    f32 = jnp.float32
    nrm = lambda k, shape, s: jax.random.normal(k, shape, f32) * s
    return {
        "x": nrm(ks[0], (BATCH, SEQ, D_MODEL), 1.0),
        "c": nrm(ks[1], (BATCH, D_MODEL), 1.0),
        "mod_w": nrm(ks[2], (DEPTH, D_MODEL, 6 * D_MODEL), 0.5 * D_MODEL ** -0.5),
        "mod_b": nrm(ks[3], (DEPTH, 6 * D_MODEL), 0.02),
        "norm1_g": 1.0 + nrm(ks[4], (DEPTH, D_MODEL), 0.02),
        "norm2_g": 1.0 + nrm(ks[5], (DEPTH, D_MODEL), 0.02),
        "w_in": nrm(ks[6], (DEPTH, D_MODEL, IN_COLS), D_MODEL ** -0.5),
        "b_f": jax.random.uniform(ks[7], (DEPTH, N_ATTN_HEADS), f32, 1.0, 6.0),
        "pool_w": nrm(ks[8], (DEPTH, len(POOL_WINDOWS), POOL_C, POOL_C), POOL_C ** -0.5),
        "pool_scale": 1.0 + nrm(ks[9], (DEPTH, POOL_W), 0.1),
        "w_out": nrm(ks[10], (DEPTH, ATTN_W + POOL_W, D_MODEL), (ATTN_W + POOL_W) ** -0.5),
        "ffn_up": nrm(ks[11], (DEPTH, D_MODEL, 2 * D_FF), D_MODEL ** -0.5),
        "ffn_conv_w": nrm(ks[12], (DEPTH, CONV_W, 2 * D_FF), CONV_W ** -0.5),
        "ffn_conv_b": nrm(ks[13], (DEPTH, 2 * D_FF), 0.02),
        "ffn_down": nrm(ks[14], (DEPTH, D_FF, D_MODEL), D_FF ** -0.5),
        "final_g": 1.0 + nrm(ks[15], (D_MODEL,), 0.02),
    }


def reference(x, c, mod_w, mod_b, norm1_g, norm2_g, w_in, b_f, pool_w, pool_scale,
              w_out, ffn_up, ffn_conv_w, ffn_conv_b, ffn_down, final_g):
    B, S, _ = x.shape
    c_act = jax.nn.silu(c)
    splits = [ATTN_W, 2 * ATTN_W, 3 * ATTN_W, 3 * ATTN_W + N_ATTN_HEADS]
    for l in range(DEPTH):
        mod = (c_act @ mod_w[l] + mod_b[l])[:, None, :]
        sh1, sc1, g1, sh2, sc2, g2 = jnp.split(mod, 6, axis=-1)

        h = _rmsnorm(x, norm1_g[l]) * (1.0 + sc1) + sh1
        z = h @ w_in[l]
        q, k, v, f_logit, u = jnp.split(z, splits, axis=-1)
        log_f = jax.nn.log_sigmoid((f_logit + b_f[l]).astype(jnp.float32))
        hd = (B, S, N_ATTN_HEADS, HEAD_DIM)
        attn = _fox_attention(q.reshape(hd), k.reshape(hd), v.reshape(hd), log_f)
        pool = _multiscale_pool(u, pool_w[l], pool_scale[l])
        mixed = jnp.concatenate([attn.astype(x.dtype), pool], axis=-1)
        x = x + g1 * (mixed @ w_out[l])

        h = _rmsnorm(x, norm2_g[l]) * (1.0 + sc2) + sh2
        x = x + g2 * _conv_ffn(h, ffn_up[l], ffn_conv_w[l], ffn_conv_b[l], ffn_down[l])
    return _rmsnorm(x, final_g)
```

```python
import numpy as np
import concourse.bass as bass
import concourse.mybir as mybir
from concourse.bass_utils import run_bass_kernel_spmd

F32 = mybir.dt.float32
BF16 = mybir.dt.bfloat16
U8 = mybir.dt.uint8
AF = mybir.ActivationFunctionType
ALU = mybir.AluOpType

D = 1024
S = 2048
L = 2
NH = 8
DH = 64
DFF = 2816
KC = 8
NB = 4
NT = 16
INCOLS = 2056
EPS = 1e-6
NPAIR = 11
WINDOWS = (2, 4, 8, 16)
MASKNEG = -30000.0
USZ = 2048
import os
FFNDBG = os.environ.get('FFNDBG', '')


def unit_plan(nlayers=L):
    plan = []
    for u in range(8):
        plan.append(("mod", 0, u))
    for l in range(nlayers):
        nm0 = [8]

        def extra(k):
            if l == 0:
                for _ in range(k):
                    if nm0[0] < 24:
                        plan.append(("mod", 0, nm0[0]))
                        nm0[0] += 1
        for u in range(4):
            plan.append(("qk", l, u))
            extra(2)
        for u in range(2):
            plan.append(("v", l, u))
            extra(2)
        plan.append(("f", l, 0))
        extra(2)
        for u in range(2):
            plan.append(("u", l, u))
            extra(2)
        assert l > 0 or nm0[0] == 24
        plan.append(("poolw", l, 0))
        for u in range(4):
            plan.append(("wop", l, u))
        for u in range(4):
            plan.append(("woa", l, u))
        nmod = 0
        for hh in range(2):
            for p in range(NPAIR):
                plan.append(("up", l, hh * NPAIR + p))
                if l + 1 < nlayers and nmod < 24:
                    plan.append(("mod", l + 1, nmod))
                    nmod += 1
            for dc in range(8):
                plan.append(("down", l, hh * 8 + dc))
                if l + 1 < nlayers and nmod < 24:
                    plan.append(("mod", l + 1, nmod))
                    nmod += 1
        assert l + 1 >= nlayers or nmod == 24
    return plan


def unit_size(kind):
    return {"mod": 2048, "qk": 2048, "v": 2048, "f": 64, "u": 2048, "poolw": 512,
            "wop": 1024, "woa": 2048, "up": 2048, "down": NPAIR * 128}[kind]


def unit_array(kind, l, u, w):
    def kcp(mat):
        return mat.reshape(KC, 128, mat.shape[1]).transpose(1, 0, 2)
    if kind == "mod":
        a = kcp(w["mod_w"][l][:, u * 256:(u + 1) * 256])
    elif kind == "qk":
        a = kcp(w["w_in"][l][:, u * 256:(u + 1) * 256])
    elif kind == "v":
        a = kcp(w["w_in"][l][:, 1024 + u * 256:1024 + (u + 1) * 256])
    elif kind == "f":
        a = kcp(w["w_in"][l][:, 1536:1544])
    elif kind == "u":
        a = kcp(w["w_in"][l][:, 1544 + u * 256:1544 + (u + 1) * 256])
    elif kind == "poolw":
        a = w["pool_w"][l].transpose(1, 0, 2)
    elif kind == "wop":
        m = w["w_out"][l][512:1024, u * 256:(u + 1) * 256]
        a = m.reshape(4, 128, 256).transpose(1, 0, 2)
    elif kind == "woa":
        m = w["w_out"][l][0:512, u * 256:(u + 1) * 256]
        a = np.zeros((128, 8, 256), np.float32)
        a[0:64] = m.reshape(8, 64, 256).transpose(1, 0, 2)
    elif kind == "up":
        hh, p = divmod(u, NPAIR)
        cg = hh * NPAIR + p
        cv = 22 + cg
        m = np.concatenate([w["ffn_up"][l][:, cg * 128:(cg + 1) * 128],
                            w["ffn_up"][l][:, cv * 128:(cv + 1) * 128]], axis=1)
        a = kcp(m)
    elif kind == "down":
        hh, dc = divmod(u, 8)
        m = w["ffn_down"][l][hh * NPAIR * 128:(hh + 1) * NPAIR * 128, dc * 128:(dc + 1) * 128]
        a = m.reshape(NPAIR, 128, 128).transpose(1, 0, 2)
    else:
        raise ValueError(kind)
    return np.ascontiguousarray(a, dtype=np.float32).reshape(128, -1)


def smalls_layout(nlayers=L):
    lay = {}
    off = 0

    def put(name, n):
        nonlocal off
        lay[name] = (off, n)
        off += n
    put("c", 8)
    put("final_g", 8)
    for l in range(nlayers):
        put(f"modb{l}", 48)
        put(f"n1g{l}", 8)
        put(f"n2g{l}", 8)
        put(f"pscale{l}", 4)
        put(f"cw{l}", 3 * 44)
        put(f"cb{l}", 44)
        put(f"bf{l}", 128)
    return lay, off


C_IDENT = 0
C_TRI = 128
C_MASK = 256
C_BAND = 384
NCST = 384 + 12 * 128


def build_consts():
    c = np.zeros((128, NCST), np.float32)
    idx = np.arange(128)
    c[:, C_IDENT:C_IDENT + 128] = np.eye(128, dtype=np.float32)
    tri = (idx[None, :] >= idx[:, None]).astype(np.float32)
    c[:, C_TRI:C_TRI + 128] = tri
    c[:, C_MASK:C_MASK + 128] = (1.0 - tri) * MASKNEG
    for g, w in enumerate(WINDOWS):
        diag = np.zeros((128, 128), np.float32)
        off = np.zeros((128, 128), np.float32)
        diag0 = np.zeros((128, 128), np.float32)
        for t in range(128):
            for k in range(w):
                tp = t - k
                if tp >= 0:
                    diag[tp, t] += 1.0 / w
                else:
                    off[128 + tp, t] += 1.0 / w
            diag[t, t] -= 1.0
            cnt = min(t + 1, w)
            for k in range(cnt):
                diag0[t - k, t] += 1.0 / cnt
            diag0[t, t] -= 1.0
        b = C_BAND + g * 384
        c[:, b:b + 128] = diag
        c[:, b + 128:b + 256] = off
        c[:, b + 256:b + 384] = diag0
    return c


class Op:
    __slots__ = ("eng", "fn", "deps", "sig", "sem", "val", "dma")

    def __init__(self, eng, fn, dma):
        self.eng = eng
        self.fn = fn
        self.dma = dma
        self.deps = ()
        self.sig = False
        self.sem = None
        self.val = 0


class Sched:
    ENGS = ("pe", "act", "dve", "pool", "sp")

    def __init__(self, nc, stack):
        self.nc = nc
        self.stack = stack
        self.ops = []
        self.last_w = {}
        self.readers = {}
        self.handles = {"pe": nc.tensor, "act": nc.scalar, "dve": nc.vector,
                        "pool": nc.gpsimd, "sp": nc.sync}
        self.dma_slots = {}

    def add(self, eng, fn, r=(), w=(), dma=None):
        o = Op(eng, fn, dma)
        deps = set()
        psr_ = [x for x in r if isinstance(x, tuple) and x[0] == "PS"]
        if psr_:
            r = [x for x in r if not (isinstance(x, tuple) and x[0] == "PS")]
            w = list(w) + psr_
        for x in r:
            p = self.last_w.get(x)
            if p is not None:
                deps.add(p)
        for x in w:
            p = self.last_w.get(x)
            if p is not None:
                deps.add(p)
            rd = self.readers.get(x)
            if rd:
                deps.update(rd.values())
        for x in r:
            d = self.readers.setdefault(x, {})
            key = eng if eng != "sp" else ("sp", len(self.ops))
            d[key] = o
        for x in w:
            self.last_w[x] = o
            self.readers[x] = {}
        o.deps = [d for d in deps if not (d.eng == "pe" and eng == "pe")]
        self.ops.append(o)
        return o

    def emit(self):
        nc = self.nc
        for o in self.ops:
            for d in o.deps:
                d.sig = True
        engsem = {}
        for e in self.ENGS:
            engsem[e] = self.stack.enter_context(nc.semaphore("sem_" + e))
        dmasem = {}
        dmacnt = {}
        cnt = {e: 0 for e in self.ENGS}
        for o in self.ops:
            if o.dma is not None:
                if o.dma not in dmasem:
                    dmasem[o.dma] = self.stack.enter_context(nc.semaphore("dma_" + str(o.dma)))
                    dmacnt[o.dma] = 0
                dmacnt[o.dma] += 16
                o.sem = dmasem[o.dma]
                o.val = dmacnt[o.dma]
            elif o.sig:
                cnt[o.eng] += 1
                o.sem = engsem[o.eng]
                o.val = cnt[o.eng]
        waited = {e: {} for e in self.ENGS}
        nwait = 0
        for o in self.ops:
            E = self.handles[o.eng]
            need = {}
            for d in o.deps:
                k = id(d.sem)
                if k not in need or need[k][1] < d.val:
                    need[k] = (d.sem, d.val)
            wd = waited[o.eng]
            for k, (sem, val) in need.items():
                if wd.get(k, 0) < val:
                    E.wait_ge(sem, val)
                    wd[k] = val
                    nwait += 1
            ins = o.fn(E)
            if o.dma is not None:
                ins.then_inc(o.sem, 16)
            elif o.sig:
                ins.then_inc(o.sem, 1)
        self.final = (dmasem, dmacnt)
        self.stats = (len(self.ops), nwait, dict(cnt))


def MM(out, lhsT, rhs, start, stop):
    return lambda E: E.matmul(out, lhsT, rhs, start=start, stop=stop)


def ACTF(out, in_, func, bias=None, scale=None):
    def f(E):
        kw = {}
        if bias is not None:
            kw["bias"] = bias
        if scale is not None:
            kw["scale"] = scale
        return E.activation(out, in_, func, **kw)
    return f


def TT(out, a, b, op):
    return lambda E: E.tensor_tensor(out, a, b, op)


def TS(out, a, s1, op0, s2=None, op1=None):
    if op1 is None:
        return lambda E: E.tensor_scalar(out, a, s1, None, op0)
    return lambda E: E.tensor_scalar(out, a, s1, s2, op0, op1)


def STT(out, a, s, b, op0, op1):
    return lambda E: E.scalar_tensor_tensor(out, a, s, b, op0, op1)


def CP(out, a):
    return lambda E: E.tensor_copy(out, a)


def MS(ap, v):
    return lambda E: E.memset(ap, v)


def DMA(out, in_):
    return lambda E: E.dma_start(out=out, in_=in_)


def build_program(nlayers=L, dbg=None, stack=None, stop=None):
    dbg = dbg or set()
    nc = bass.Bass("TRN2", target_bir_lowering=False)
    plan = unit_plan(nlayers)
    offs = []
    tot = 0
    for (k, l, u) in plan:
        offs.append(tot)
        tot += unit_size(k)
    WTOT = tot
    slay, NS = smalls_layout(nlayers)

    xT = nc.dram_tensor("xT", [D, S], F32, kind="ExternalInput").ap()
    Wd = nc.dram_tensor("W", [128, WTOT], F32, kind="ExternalInput").ap()
    smd = nc.dram_tensor("smalls", [128, NS], F32, kind="ExternalInput").ap()
    cstd = nc.dram_tensor("consts", [128, NCST], F32, kind="ExternalInput").ap()
    outT = nc.dram_tensor("outT", [D, S], F32, kind="ExternalOutput").ap()
    dbg_out = {}

    SC = Sched(nc, stack)

    LIMIT = 229344
    base0 = (nc.sbuf_base + 31) // 32 * 32
    cur = [base0]

    def alloc(name, shape, dtype):
        esz = 4 if dtype == F32 else 2
        n = esz
        for s_ in shape[1:]:
            n *= s_
        off = cur[0]
        cur[0] += (n + 31) // 32 * 32
        assert cur[0] <= LIMIT, (name, cur[0], LIMIT)
        return nc.alloc_sbuf_tensor_at(name, list(shape), dtype, offset=off)

    def alloc_at(name, shape, dtype, off):
        return nc.alloc_sbuf_tensor_at(name, list(shape), dtype, offset=off)

    X = alloc("X", [128, KC, S], F32)
    H = alloc("H", [128, KC, S], BF16)
    ar0 = cur[0]
    QT = alloc("QT", [128, 4, S], BF16)
    KT = alloc("KT", [128, 4, S], BF16)
    V = alloc("V", [128, NT, NH, 65], BF16)
    HID = alloc_at("HID", [128, NPAIR, S], BF16, ar0)
    assert NPAIR * S * 2 <= cur[0] - ar0
    po0 = cur[0]
    PO_SIZE = 16640
    cur[0] += PO_SIZE
    POOLOUT = alloc_at("POOLOUT", [128, 4, S], BF16, po0)
    KX = [alloc_at(f"KX{b}", [128, S], BF16, po0 + b * 4096) for b in range(2)]
    G = [alloc_at(f"G{b}", [128, NT, 67], BF16, po0 + 8192 + b * 2560) for b in range(2)]
    NPT = 3
    PT = [alloc_at(f"PT{k}", [128, 512], BF16, po0 + 13312 + k * 1024) for k in range(NPT)]
    NT0 = 3
    T0ALL = alloc_at("T0ALL", [128, 2 * NT0, 512], F32, po0)
    SG = [alloc_at(f"SG{k}", [128, 512], BF16, po0 + 12288 + k * 1024) for k in range(2)]
    HALO = [alloc_at(f"HALO{k}", [128, 4], F32, po0 + 14336 + k * 32) for k in range(4)]
    TMPA = [alloc_at(f"TMPA{k}", [128, 2], F32, po0 + 14464 + k * 32) for k in range(4)]
    HGV = [alloc_at(f"HGV{k}", [128, 2, 2], F32, po0 + 14720 + k * 32) for k in range(2)]
    HBG = [alloc_at(f"HBG{k}", [128, 2, 1], F32, po0 + 14784 + k * 32) for k in range(2)]
    TMPB = [alloc_at(f"TMPB{k}", [128, 2], F32, po0 + 14592 + k * 32) for k in range(4)]

    def po(off, n):
        return [("PO", p) for p in range(off // 512, (off + n - 1) // 512 + 1)]
    sc0 = cur[0]
    SC_SIZE = 13312
    cur[0] += SC_SIZE
    SQ = [alloc_at(f"SQ{k}", [128, 512], BF16, sc0 + k * 1024) for k in range(2)]
    RSTD = [alloc_at(f"RSTD{k}", [128, 512], F32, sc0 + 2048 + k * 2048) for k in range(2)]
    TN = [alloc_at(f"TN{k}", [128, 512], F32, sc0 + 6144 + k * 2048) for k in range(2)]
    UTR = [alloc_at(f"UTR{k}", [128, 256], BF16, sc0 + k * 512) for k in range(3)]
    MTR = [alloc_at(f"MTR{k}", [128, 512], BF16, sc0 + 1536 + k * 1024) for k in range(2)]
    NLt = alloc_at("NL", [128, 128], F32, sc0 + 3584)
    TOTt = alloc_at("TOT", [128, NT, NH], F32, sc0 + 4096)
    OFFt = alloc_at("OFF", [128, NT, NH], F32, sc0 + 4608)
    NEGt = alloc_at("NEG", [128, 128], F32, sc0 + 5120)
    R1t = alloc_at("R1", [128, 128], F32, sc0 + 5632)
    OSB = [alloc_at(f"OSB{k}", [128, 512], F32, sc0 + k * 2048) for k in range(2)]
    RR = [alloc_at(f"RR{k}", [128, 512], F32, sc0 + 4096) for k in range(1)]
    QX = [alloc_at(f"QX{k}", [128, 512], BF16, sc0 + (6144 if k < 2 else 12288 - 2048) + k * 1024) for k in range(3)]
    RHI = [alloc_at(f"RHI{k}", [128, 512], BF16, sc0 + 8192 + k * 2048) for k in range(2)]
    RLO = [alloc_at(f"RLO{k}", [128, 512], BF16, sc0 + 8192 + 1024 + k * 2048) for k in range(2)]

    def sc(off, n):
        return [("SC", p) for p in range(off // 512, (off + n - 1) // 512 + 1)]
    NWB = 6
    WBF = [alloc(f"WBF{k}", [128, USZ], BF16) for k in range(NWB)]
    CSTB = alloc("CSTB", [128, NCST], BF16)
    ONESB = alloc("ONESB", [128, 128], BF16)
    TRIF = alloc("TRIF", [128, 128], F32)
    ONESF = alloc("ONESF", [128, 128], F32)
    SEL = alloc("SEL", [128, 64], F32)
    SM = alloc("SM", [128, NS], F32)
    CACT = alloc("CACT", [128, 8], F32)
    CACTB = alloc("CACTB", [128, 8], BF16)
    SELB = alloc("SELB", [128, 64], BF16)
    MODT = [alloc(f"MODT{l}", [128, 48], F32) for l in range(nlayers)]
    AA = [alloc(f"AA{l}", [128, 16], F32) for l in range(nlayers)]
    NFT = alloc("NFT", [128, NT, NH], F32)
    HIt = alloc("HI", [128, NT, NH], BF16)
    MIDt = alloc("MID", [128, NT, NH], BF16)
    LOt = alloc("LO", [128, NT, NH], BF16)
    sbuf_used = cur[0] - base0
    nc.alloc_sbuf_tensor("slab", [128, cur[0] - nc.sbuf_base], U8)

    PS = nc.alloc_psum_tensor("PS", [128, 8, 512], F32)

    def psr(b):
        return [("PS", b)]

    def sm(name, a=0, n=None):
        o, ln = slay[name]
        if n is None:
            n = ln - a
        return SM[:, o + a:o + a + n]

    IDENTB = CSTB[:, C_IDENT:C_IDENT + 128]
    MASKB = CSTB[:, C_MASK:C_MASK + 128]

    def band(g, k):
        b = C_BAND + g * 384 + k * 128
        return CSTB[:, b:b + 128]

    add = SC.add

    def dump(name, ap, res, shape, dtype):
        if name not in dbg:
            return
        t = nc.dram_tensor("dbg_" + name, list(shape), dtype, kind="ExternalOutput").ap()
        dbg_out[name] = t
        add("sp", DMA(t, ap), r=res, dma="dbg_" + name)

    add("sp", DMA(SM[:, :], smd), w=["SM"], dma="sm")
    add("pool", DMA(CSTB[:, :], cstd), w=["CSTB"], dma="cst")
    add("sp", DMA(TRIF[:, :], cstd[:, C_TRI:C_TRI + 128]), w=["TRIF"], dma="trif")
    add("pool", MS(ONESB[:, :], 1.0), w=["ONESB"])
    add("pool", MS(ONESF[:, :], 1.0), w=["ONESF"])
    add("pool", MS(SEL[:, :], 0.0), w=["SEL"])
    add("pool", MS(SEL[64:65, :], 1.0), w=["SEL"])
    for c in range(KC):
        add("sp", DMA(X[:, c, :], xT[c * 128:(c + 1) * 128, :]),
            w=[("X", c, b) for b in range(NB)], dma=("x", c))
    add("act", ACTF(CACT[:, :], sm("c"), AF.Silu), r=["SM"], w=["CACT"])
    add("dve", CP(CACTB[:, :], CACT[:, :]), r=["CACT"], w=["CACTB"])
    add("pool", MS(SELB[:, :], 0.0), w=["SELB"])
    add("pool", MS(SELB[64:65, :], 1.0), w=["SELB"])

    class WS:
        pass
    ws = WS()
    ws.loaded = 0
    ws.cast = {}
    ws.ncast = 0
    ws.lru = [0, 1, 2]
    ws.ffn = False

    def wbf_res(b):
        return [("WBF", b)] if b < 2 else sc(0, 4096)
    ws.pos = 0
    nunits = len(plan)

    def ws_load(n):
        k, l, u = plan[n]
        sz = unit_size(k)
        slot = n % NWB
        pp = 128
        add("pool", DMA(WBF[slot][0:pp, 0:sz], Wd[0:pp, offs[n]:offs[n] + sz]),
            w=[("WBF", slot)], dma=("wbf", slot))

    def ws_ensure_load(n):
        while ws.loaded <= n and ws.loaded < nunits:
            ws_load(ws.loaded)
            ws.loaded += 1

    def ws_get(kind, l, u):
        n = ws.pos
        assert plan[n] == (kind, l, u), (plan[n], kind, l, u)
        ws_ensure_load(min(n + NWB - 1, nunits - 1))
        b = n % NWB
        return WBF[b], [("WBF", b)]

    def ws_peek(kind, l, u, k):
        n = ws.pos + k
        assert plan[n] == (kind, l, u), (plan[n], kind, l, u)
        ws_ensure_load(min(ws.pos + NWB - 1, nunits - 1))
        b = n % NWB
        return WBF[b], [("WBF", b)]

    def ws_done(drain=True):
        ws.pos += 1
        if drain:
            while ws.pos < nunits and plan[ws.pos][0] == "mod":
                mod_unit(plan[ws.pos][1], plan[ws.pos][2])

    MODBANK = 6

    def mod_unit(l, u):
        wt, wr = ws_get("mod", l, u)
        w3 = wt[:, :].rearrange("p (k n) -> p k n", k=KC)
        for cc in range(2):
            for kc in range(KC):
                add("pe", MM(PS[:, MODBANK, cc:cc + 1], w3[:, kc, cc * 128:(cc + 1) * 128],
                             CACTB[:, kc:kc + 1], kc == 0, kc == KC - 1),
                    r=wr + ["CACTB"], w=psr(MODBANK))
        o_ = slay[f"modb{l}"][0]
        add("dve", TT(MODT[l][:, 2 * u:2 * u + 2], PS[:, MODBANK, 0:2], SM[:, o_ + 2 * u:o_ + 2 * u + 2], ALU.add),
            r=psr(MODBANK) + ["SM"], w=[("MODT", l, u // 4, u % 4)])
        ws_done(drain=False)

    def MODr(l, part):
        return [("MODT", l, part, k) for k in range(4)]

    def Xr(c, b):
        return [("X", c, b)]

    def Hr(c, b):
        return [("H", c, b)]

    NORMBANKS = (6, 7)
    nrm = {"n": 0}

    def rmsnorm_stats(b):
        k = nrm["n"] % 2
        nrm["n"] += 1
        bank = NORMBANKS[k]
        for c in range(KC):
            q = c % 2
            add("act", ACTF(SQ[q][:, :], X[:, c, b * 512:(b + 1) * 512], AF.Square),
                r=Xr(c, b), w=sc(q * 1024, 1024))
            add("pe", MM(PS[:, bank, :], ONESB[:, :], SQ[q][:, :], c == 0, c == KC - 1),
                r=sc(q * 1024, 1024) + ["ONESB"], w=psr(bank))
        rres = sc(2048 + k * 2048, 2048)
        add("act", ACTF(RSTD[k][:, :], PS[:, bank, :], AF.Ln, bias=EPS, scale=1.0 / D),
            r=psr(bank), w=rres)
        add("act", ACTF(RSTD[k][:, :], RSTD[k][:, :], AF.Exp, scale=-0.5), r=rres, w=rres)
        return RSTD[k], rres

    def norm_prologue(l, which):
        aoff = 0 if which == 1 else 8
        scp = 1 if which == 1 else 4
        add("dve", STT(AA[l][:, aoff:aoff + 8], MODT[l][:, scp * 8:scp * 8 + 8], 1.0,
                       sm(f"n1g{l}" if which == 1 else f"n2g{l}"), ALU.add, ALU.mult),
            r=MODr(l, scp) + ["SM"], w=[("AA", l, which)])

    def norm_block(l, which, b):
        aoff = 0 if which == 1 else 8
        shp = 0 if which == 1 else 3
        shoff = 0 if which == 1 else 24
        rs, rres = rmsnorm_stats(b)
        for c in range(KC):
            q = c % 2
            tres = sc(6144 + q * 2048, 2048)
            add("dve", TT(TN[q][:, :], X[:, c, b * 512:(b + 1) * 512], rs[:, :], ALU.mult),
                r=Xr(c, b) + rres, w=tres)
            add("act", ACTF(H[:, c, b * 512:(b + 1) * 512], TN[q][:, :], AF.Identity,
                            bias=MODT[l][:, shoff + c:shoff + c + 1], scale=AA[l][:, aoff + c:aoff + c + 1]),
                r=tres + MODr(l, shp) + [("AA", l, which)], w=Hr(c, b))

    def norm_mod(l, which):
        norm_prologue(l, which)
        for b in range(NB):
            norm_block(l, which, b)

    rot = {"a": 0, "b": 0}

    def bankA():
        b = rot["a"] % 4
        rot["a"] += 1
        return b

    def bankB():
        b = 4 + rot["b"] % 2
        rot["b"] += 1
        return b

    def QTr(c, b):
        return [("AR", c * 4 + b)]

    def KTr(c, b):
        return [("AR", 16 + c * 4 + b)]

    def Vr(i):
        o = 32768 + i * 1040
        return [("AR", p) for p in range(o // 1024, (o + 1039) // 1024 + 1)]

    def HIDr(kc, b):
        return [("AR", kc * 4 + b)]

    def proj_qk(l):
        ev = 0
        for u in range(4):
            wt, wr = ws_get("qk", l, u)
            w3 = wt[:, :].rearrange("p (k n) -> p k n", k=KC)
            for cc in range(2):
                ch = (u % 2) * 2 + cc
                for b in range(NB):
                    bank = bankA()
                    for kc in range(KC):
                        add("pe", MM(PS[:, bank, :], w3[:, kc, cc * 128:(cc + 1) * 128],
                                     H[:, kc, b * 512:(b + 1) * 512], kc == 0, kc == KC - 1),
                            r=wr + Hr(kc, b), w=psr(bank))
                    if u < 2:
                        add("act", ACTF(QT[:, ch, b * 512:(b + 1) * 512], PS[:, bank, :], AF.Identity, scale=0.125),
                            r=psr(bank), w=QTr(ch, b))
                    else:
                        add("dve", CP(KT[:, ch, b * 512:(b + 1) * 512], PS[:, bank, :]),
                            r=psr(bank), w=KTr(ch, b))
            ws_done()

    def proj_v(l):
        for u in range(2):
            wt, wr = ws_get("v", l, u)
            w3 = wt[:, :].rearrange("p (k n) -> p k n", k=KC)
            for i in range(NT):
                bank = bankB()
                for kc in range(KC):
                    add("pe", MM(PS[:, bank, 0:256], H[:, kc, i * 128:(i + 1) * 128], w3[:, kc, :],
                                 kc == 0, kc == KC - 1),
                        r=wr + Hr(kc, i // 4), w=psr(bank))
                src = PS[:, bank, 0:256].rearrange("p (h d) -> p h d", h=4)
                dst = V[:, i, u * 4:(u + 1) * 4, 0:64]
                eng = "act" if i % 2 == 0 else "dve"
                if eng == "act":
                    add("act", ACTF(dst, src, AF.Identity), r=psr(bank), w=Vr(i))
                else:
                    add("dve", CP(dst, src), r=psr(bank), w=Vr(i))
            ws_done()

    FBANK = 6

    def proj_f(l):
        wt, wr = ws_get("f", l, 0)
        w3 = wt[:, 0:64].rearrange("p (k n) -> p k n", k=KC)
        for i in range(NT):
            for kc in range(KC):
                add("pe", MM(PS[:, FBANK, i * 8:(i + 1) * 8], H[:, kc, i * 128:(i + 1) * 128], w3[:, kc, :],
                             kc == 0, kc == KC - 1),
                    r=wr + Hr(kc, i // 4), w=psr(FBANK))
        nlr = sc(3584, 512)
        add("dve", TT(NLt[:, :], PS[:, FBANK, 0:128], sm(f"bf{l}"), ALU.add), r=psr(FBANK) + ["SM"], w=nlr)
        add("act", ACTF(NLt[:, :], NLt[:, :], AF.Exp, scale=-1.0), r=nlr, w=nlr)
        add("act", ACTF(NLt[:, :], NLt[:, :], AF.Ln, bias=1.0), r=nlr, w=nlr)
        add("pe", MM(PS[:, FBANK, 128:256], TRIF[:, :], NLt[:, :], True, True), r=nlr + ["TRIF"], w=psr(FBANK))
        add("pe", MM(PS[:, FBANK, 256:384], ONESF[:, :], NLt[:, :], True, True), r=nlr + ["ONESF"], w=psr(FBANK))
        totr = sc(4096, 512)
        offr = sc(4608, 512)
        add("dve", CP(TOTt[:, :, :], PS[:, FBANK, 256:384].rearrange("p (i h) -> p i h", h=NH)),
            r=psr(FBANK), w=totr)
        add("pool", MS(OFFt[:, 0, :], 0.0), w=offr)
        for i in range(1, NT):
            add("dve", TT(OFFt[:, i, :], OFFt[:, i - 1, :], TOTt[:, i - 1, :], ALU.add),
                r=totr + offr, w=offr)
        add("dve", TT(NFT[:, :, :], PS[:, FBANK, 128:256].rearrange("p (i h) -> p i h", h=NH), OFFt[:, :, :], ALU.add),
            r=psr(FBANK) + offr, w=["NFT"])
        negr = sc(5120, 512)
        r1r = sc(5632, 512)
        nft2 = NFT[:, :, :].rearrange("p i h -> p (i h)")
        hi2 = HIt[:, :, :].rearrange("p i h -> p (i h)")
        mid2 = MIDt[:, :, :].rearrange("p i h -> p (i h)")
        lo2 = LOt[:, :, :].rearrange("p i h -> p (i h)")
        add("dve", TS(NEGt[:, :], nft2, -1.0, ALU.mult), r=["NFT"], w=negr)
        add("dve", CP(hi2, NEGt[:, :]), r=negr, w=["HI"])
        add("dve", TT(R1t[:, :], NEGt[:, :], hi2, ALU.subtract), r=negr + ["HI"], w=r1r)
        add("dve", CP(mid2, R1t[:, :]), r=r1r, w=["MID"])
        add("dve", TT(NEGt[:, :], R1t[:, :], mid2, ALU.subtract), r=r1r + ["MID"], w=negr)
        add("dve", CP(lo2, NEGt[:, :]), r=negr, w=["LO"])
        ws_done()

    def POr(g, b):
        return po(g * 4096 + b * 1024, 1024)

    def proj_u_pool(l):
        for u in range(2):
            wt, wr = ws_get("u", l, u)
            w3 = wt[:, :].rearrange("p (k n) -> p k n", k=KC)
            prev = None
            bankM = [None, None]
            for i in range(NT):
                bank = bankB()
                for kc in range(KC):
                    add("pe", MM(PS[:, bank, 0:256], H[:, kc, i * 128:(i + 1) * 128], w3[:, kc, :],
                                 kc == 0, kc == KC - 1),
                        r=wr + Hr(kc, i // 4), w=psr(bank))
                k = i % 3
                utres = sc(k * 512, 512)
                if i % 2 == 0:
                    add("act", ACTF(UTR[k][:, :], PS[:, bank, 0:256], AF.Identity), r=psr(bank), w=utres)
                else:
                    add("dve", CP(UTR[k][:, :], PS[:, bank, 0:256]), r=psr(bank), w=utres)
                for gg in range(2):
                    g = u * 2 + gg
                    if i % 4 == 0:
                        bankM[gg] = bankA()
                    bm = bankM[gg]
                    o_ap = PS[:, bm, (i % 4) * 128:(i % 4 + 1) * 128]
                    if i == 0:
                        add("pe", MM(o_ap, UTR[k][:, gg * 128:(gg + 1) * 128], band(g, 2), True, True),
                            r=utres + ["CSTB"], w=psr(bm))
                    else:
                        pk, pres = prev
                        add("pe", MM(o_ap, UTR[pk][:, gg * 128:(gg + 1) * 128], band(g, 1), True, False),
                            r=pres + ["CSTB"], w=psr(bm))
                        add("pe", MM(o_ap, UTR[k][:, gg * 128:(gg + 1) * 128], band(g, 0), False, True),
                            r=utres + ["CSTB"], w=psr(bm))
                    if i % 4 == 3:
                        b = i // 4
                        if gg == 0:
                            add("act", ACTF(POOLOUT[:, g, b * 512:(b + 1) * 512], PS[:, bm, :], AF.Identity),
                                r=psr(bm), w=POr(g, b))
                        else:
                            add("dve", CP(POOLOUT[:, g, b * 512:(b + 1) * 512], PS[:, bm, :]),
                                r=psr(bm), w=POr(g, b))
                prev = (k, utres)
            ws_done()
        wt, wr = ws_get("poolw", l, 0)
        w3 = wt[:, 0:512].rearrange("p (g d) -> p g d", g=4)
        for g in range(4):
            for b in range(NB):
                bank = bankA()
                add("pe", MM(PS[:, bank, :], w3[:, g, :], POOLOUT[:, g, b * 512:(b + 1) * 512], True, True),
                    r=wr + POr(g, b), w=psr(bank))
                ps_ap = sm(f"pscale{l}", g, 1)
                if (g + b) % 2 == 0:
                    add("act", ACTF(POOLOUT[:, g, b * 512:(b + 1) * 512], PS[:, bank, :], AF.Identity, scale=ps_ap),
                        r=psr(bank) + ["SM"], w=POr(g, b))
                else:
                    add("dve", TS(POOLOUT[:, g, b * 512:(b + 1) * 512], PS[:, bank, :], ps_ap, ALU.mult),
                        r=psr(bank) + ["SM"], w=POr(g, b))
        ws_done()

    def resid_update(l, bank, dchunk, b, gcol):
        add("dve", STT(X[:, dchunk, b * 512:(b + 1) * 512], PS[:, bank, :], MODT[l][:, gcol + dchunk:gcol + dchunk + 1],
                       X[:, dchunk, b * 512:(b + 1) * 512], ALU.mult, ALU.add),
            r=psr(bank) + MODr(l, gcol // 8) + Xr(dchunk, b), w=Xr(dchunk, b))

    def wout_pool(l):
        for u in range(4):
            wt, wr = ws_get("wop", l, u)
            w3 = wt[:, 0:1024].rearrange("p (g n) -> p g n", g=4)
            for cc in range(2):
                dch = u * 2 + cc
                for b in range(NB):
                    bank = bankA()
                    for g in range(4):
                        add("pe", MM(PS[:, bank, :], w3[:, g, cc * 128:(cc + 1) * 128],
                                     POOLOUT[:, g, b * 512:(b + 1) * 512], g == 0, g == 3),
                            r=wr + POr(g, b), w=psr(bank))
                    resid_update(l, bank, dch, b, 16)
            ws_done()

    def wout_attn(l, after_block=None):
        units = [ws_peek("woa", l, u, u) for u in range(4)]
        for b in range(NB):
            for u in range(4):
                wt, wr = units[u]
                w3 = wt[:, :].rearrange("p (h n) -> p h n", h=NH)
                for cc in range(2):
                    dch = u * 2 + cc
                    bank = bankA()
                    for h in range(NH):
                        add("pe", MM(PS[:, bank, :], w3[:, h, cc * 128:(cc + 1) * 128],
                                     H[:, h, b * 512:(b + 1) * 512], h == 0, h == NH - 1),
                            r=wr + Hr(h, b), w=psr(bank))
                    resid_update(l, bank, dch, b, 16)
            if after_block is not None:
                after_block(b)
        for u in range(4):
            ws_done(drain=(u == 3))

    SBANKS = (0, 1, 2)
    ACCB = (3, 4, 6)
    BCBANK = 5
    FQB = (7, 7)

    def KXr(buf):
        return po(buf * 4096, 4096)

    def Gr(buf):
        return po(8192 + buf * 2560, 2144)

    def PTr(k):
        return po(13312 + k * 1024, 1024)

    def QXr(k):
        return sc((6144 if k < 2 else 12288 - 2048) + k * 1024, 1024)

    def attn_init():
        add("dve", MS(G[0][:, :, :], 0.0), w=Gr(0))
        add("dve", MS(KX[0][64:128, :], 0.0), w=KXr(0))
        add("dve", MS(KX[0][64:67, :], 1.0), w=KXr(0))
        for k in range(3):
            add("pool", MS(QX[k][:, :], 0.0), w=QXr(k))
        for k in range(2):
            add("pool", MS(OSB[k][64:128, :], 0.0), w=sc(k * 2048, 2048))
        add("pool", MS(G[1][:, :, :], 0.0), w=Gr(1))
        add("pool", MS(KX[1][0:64, :], 0.0), w=KXr(1))
        add("pool", MS(KX[1][0:3, :], 1.0), w=KXr(1))

    def build_head(h):
        par = h % 2
        ch = h // 2
        p0 = 64 * par
        gc = 64 * (1 - par)
        add("dve", CP(KX[par][p0:p0 + 64, :], KT[p0:p0 + 64, ch, :]),
            r=[("AR", 16 + ch * 4 + b) for b in range(NB)], w=KXr(par))
        add("dve", CP(G[par][:, :, gc], HIt[:, :, h]), r=["HI"], w=Gr(par))
        add("dve", CP(G[par][:, :, gc + 1], MIDt[:, :, h]), r=["MID"], w=Gr(par))
        add("dve", CP(G[par][:, :, gc + 2], LOt[:, :, h]), r=["LO"], w=Gr(par))

    qxs = {"n": 0, "par": [None, None, None]}

    def prep_group(h, j):
        par = h % 2
        ch = h // 2
        p0 = 64 * par
        gc = 64 * (1 - par)
        k = qxs["n"] % 3
        bank = FQB[qxs["n"] % 2]
        qxs["n"] += 1
        M = gc + 3
        for ii in range(4):
            i = j * 4 + ii
            add("pe", MM(PS[0:M, bank, ii * 128:(ii + 1) * 128], G[par][:, i, 0:M], IDENTB, True, True),
                r=Gr(par) + ["CSTB"], w=psr(bank))
        add("dve", CP(QX[k][p0:p0 + 64, :], QT[p0:p0 + 64, ch, j * 512:(j + 1) * 512]), r=QTr(ch, j), w=QXr(k))
        add("dve", CP(QX[k][gc:gc + 3, :], PS[gc:gc + 3, bank, :]), r=psr(bank), w=QXr(k))
        return k

    def attention(l):
        attn_init()
        groups = [(h, j) for h in range(NH) for j in range(NB)]
        steps = []
        for gi, (h, j) in enumerate(groups):
            for i in range(4 * j + 4):
                steps.append((gi, h, j, i))
        st = {"acc": 0, "nrm": 0}
        info = {}
        gq = {}
        qxs["par"] = [None, None, None]

        def issue_S(n):
            gi, h, j, i = steps[n]
            par = h % 2
            sb = SBANKS[n % 3]
            d = i - 4 * j
            c0 = 128 * d if d > 0 else 0
            diag = d >= 0
            qk = gq[gi]
            add("pe", MM(PS[:, sb, c0:512], KX[par][:, i * 128:(i + 1) * 128], QX[qk][:, c0:512], True, not diag),
                r=KXr(par) + QXr(qk), w=psr(sb))
            if diag:
                add("pe", MM(PS[:, sb, c0:c0 + 128], IDENTB, MASKB, False, True), r=["CSTB"], w=psr(sb))
            k = n % NPT
            add("act", ACTF(PT[k][:, c0:512], PS[:, sb, c0:512], AF.Exp, bias=NFT[:, i, h:h + 1]),
                r=psr(sb) + ["NFT"], w=PTr(k))
            info[n] = (k, c0)

        def issue_PV(n):
            gi, h, j, i = steps[n]
            k, c0 = info.pop(n)
            last = (i == 4 * j + 3)
            if i == 0:
                st["accb"] = ACCB[st["acc"] % 3]
                st["acc"] += 1
            ab = st["accb"]
            add("pe", MM(PS[0:65, ab, c0:512], V[:, i, h, 0:65], PT[k][:, c0:512], i == 0, last),
                r=Vr(i) + PTr(k), w=psr(ab))
            if last:
                q = st["nrm"] % 2
                st["nrm"] += 1
                lres = sc(8192 + q * 2048, 2048)
                rres = sc(4096, 2048)
                ores = sc(q * 2048, 2048)
                for p_ in list(pending):
                    if p_[2] == q:
                        pending.remove(p_)
                        p_[1]()
                add("dve", CP(OSB[q][0:65, :], PS[0:65, ab, :]), r=psr(ab), w=ores)

                def recip(q=q, lres=lres, ores=ores, j=j):
                    add("dve", lambda E, q=q: E.reciprocal(OSB[q][64:65, :], OSB[q][64:65, :]), r=ores, w=ores)
                pending.append([5 if j == 3 else 3, recip, q])

                def tail(q=q, h=h, j=j, lres=lres, ores=ores):
                    add("pe", MM(PS[0:64, BCBANK, :], SEL[:, :], OSB[q][:, :], True, True),
                        r=ores + ["SEL"], w=psr(BCBANK))
                    add("dve", TT(H[0:64, h, j * 512:(j + 1) * 512], OSB[q][0:64, :], PS[0:64, BCBANK, :], ALU.mult),
                        r=ores + psr(BCBANK), w=Hr(h, j))
                pending.append([DEFER, tail, q])

        pending = []
        DEFER = 10

        def tick():
            for p_ in list(pending):
                p_[0] -= 1
                if p_[0] <= 0:
                    pending.remove(p_)
                    p_[1]()

        build_head(0)
        gq[0] = prep_group(*groups[0])
        gq[1] = prep_group(*groups[1])
        N = len(steps)
        LOOK = 2
        for n in range(N + LOOK):
            if n < N:
                gi, h, j, i = steps[n]
                issue_S(n)
                if i == 0:
                    if j == 0 and h + 1 < NH:
                        build_head(h + 1)
                    if gi + 2 < len(groups):
                        gq[gi + 2] = prep_group(*groups[gi + 2])
            if n - LOOK >= 0:
                issue_PV(n - LOOK)
            tick()
        while pending:
            tick()

    def ffn(l, next_norm=None):
        cw = slay[f"cw{l}"][0]
        cb = slay[f"cb{l}"][0]

        def CW(j, c):
            return SM[:, cw + j * 44 + c:cw + j * 44 + c + 1]

        def CB(c):
            return SM[:, cb + c:cb + c + 1]
        t0 = {"n": 0}
        pend = [None]
        uniq = [("HBG", k, g_) for k in range(2) for g_ in range(2)] + [("HB", k) for k in range(4)] + [("HGV", k, g_) for k in range(2) for g_ in range(2)]
        for k in range(4):
            add("pool", MS(HALO[k][:, :], 0.0), w=po(14336, 512) + uniq)
        for hh in range(2):
            for p in range(NPAIR):
                wt, wr = ws_get("up", l, hh * NPAIR + p)
                w3 = wt[:, :].rearrange("p (k n) -> p k n", k=KC)
                cg = hh * NPAIR + p
                cv = 22 + cg
                for b in range(NB):
                    r_ = t0["n"] % 3
                    t0["n"] += 1
                    bg, bv = 2 * r_, 2 * r_ + 1
                    for (bank, cc) in ((bg, 0), (bv, 1)):
                        for kc in range(KC):
                            add("pe", MM(PS[:, bank, :], w3[:, kc, cc * 128:(cc + 1) * 128],
                                         H[:, kc, b * 512:(b + 1) * 512], kc == 0, kc == KC - 1),
                                r=wr + Hr(kc, b), w=psr(bank))
                    outs = []
                    chains = ((bg, cg, r_, r_ * 2048, 0), (bv, cv, NT0 + r_, (NT0 + r_) * 2048, 1))
                    hpar = b % 2
                    for (bank, ch, T0, toff, gv) in chains:
                        tres = po(toff, 2048)
                        add("act", ACTF(T0ALL[:, T0, :], PS[:, bank, :], AF.Identity, bias=CB(ch), scale=CW(2, ch)),
                            r=psr(bank) + ["SM"], w=tres)
                        if b < NB - 1:
                            add("act", ACTF(HGV[hpar][:, gv, 0:2], PS[:, bank, 510:512], AF.Identity, scale=CW(0, ch)),
                                r=psr(bank) + ["SM"], w=[("HGV", hpar, gv)])
                            add("act", ACTF(HBG[hpar][:, gv, 0:1], PS[:, bank, 511:512], AF.Identity, scale=CW(1, ch)),
                                r=psr(bank) + ["SM"], w=[("HBG", hpar, gv)])
                        outs.append((T0, tres))
                    if pend[0] is not None:
                        pend[0]()
                        pend[0] = None
                    for (bank, ch, T0, toff, gv) in chains:
                        tres = po(toff, 2048)
                        add("dve", STT(T0ALL[:, T0, 1:512], PS[:, bank, 0:511], CW(1, ch), T0ALL[:, T0, 1:512], ALU.mult, ALU.add),
                            r=psr(bank) + tres + ["SM"], w=tres)
                    for (bank, ch, T0, toff, gv) in chains:
                        tres = po(toff, 2048)
                        add("dve", STT(T0ALL[:, T0, 2:512], PS[:, bank, 0:510], CW(0, ch), T0ALL[:, T0, 2:512], ALU.mult, ALU.add),
                            r=psr(bank) + tres + ["SM"], w=tres)
                    if b > 0:
                        pp_ = (b + 1) % 2
                        t4 = T0ALL[:, :, :].rearrange("p (g r) n -> p g r n", g=2)
                        both = po(r_ * 2048, 2048) + po((NT0 + r_) * 2048, 2048)
                        add("dve", TT(t4[:, :, r_, 0:2], t4[:, :, r_, 0:2], HGV[pp_][:, :, :], ALU.add),
                            r=[("HGV", pp_, 0), ("HGV", pp_, 1)] + both, w=both)
                        add("dve", TT(t4[:, :, r_, 0:1], t4[:, :, r_, 0:1], HBG[pp_][:, :, :], ALU.add),
                            r=[("HBG", pp_, 0), ("HBG", pp_, 1)] + both, w=both)
                    (Tg, gres), (Tv, vres) = outs
                    q = (t0["n"]) % 2

                    def fin(q=q, Tg=Tg, gres=gres, Tv=Tv, vres=vres, p=p, b=b):
                        sgres = po(12288 + q * 1024, 1024)
                        add("act", ACTF(SG[q][:, :], T0ALL[:, Tg, :], AF.Silu), r=gres, w=sgres)
                        add("pool", TT(HID[:, p, b * 512:(b + 1) * 512], SG[q][:, :], T0ALL[:, Tv, :], ALU.mult),
                            r=sgres + vres, w=HIDr(p, b))
                    pend[0] = fin
                ws_done()
            if pend[0] is not None:
                pend[0]()
                pend[0] = None
            if hh == 0 or next_norm is None:
                for dc in range(8):
                    wt, wr = ws_get("down", l, hh * 8 + dc)
                    w3 = wt[:, 0:NPAIR * 128].rearrange("p (k n) -> p k n", k=NPAIR)
                    for b in range(NB):
                        bank = 6 + (dc * NB + b) % 2
                        for kc in range(NPAIR):
                            add("pe", MM(PS[:, bank, :], w3[:, kc, :], HID[:, kc, b * 512:(b + 1) * 512],
                                         kc == 0, kc == NPAIR - 1),
                                r=wr + HIDr(kc, b), w=psr(bank))
                        resid_update(l, bank, dc, b, 40)
                    ws_done()
            else:
                nb_ = 0
                for grp in range(2):
                    units = [ws_peek("down", l, hh * 8 + grp * 4 + k, k) for k in range(4)]
                    for b in range(NB):
                        for k in range(4):
                            dc = grp * 4 + k
                            wt, wr = units[k]
                            w3 = wt[:, 0:NPAIR * 128].rearrange("p (k n) -> p k n", k=NPAIR)
                            bank = nb_ % 4
                            nb_ += 1
                            for kc in range(NPAIR):
                                add("pe", MM(PS[:, bank, :], w3[:, kc, :], HID[:, kc, b * 512:(b + 1) * 512],
                                             kc == 0, kc == NPAIR - 1),
                                    r=wr + HIDr(kc, b), w=psr(bank))
                            resid_update(l, bank, dc, b, 40)
                        if grp == 1:
                            next_norm(b)
                    for k in range(4):
                        ws_done(drain=(k == 3))

    def ffn_close():
        uniq = [("HBG", k, g_) for k in range(2) for g_ in range(2)] + [("HGV", k, g_) for k in range(2) for g_ in range(2)]
        add("pool", MS(HALO[0][:, :], 0.0), r=uniq, w=po(14336, 512))

    def final_norm_block(b):
        rs, rres = rmsnorm_stats(b)
        for c in range(KC):
            q = c % 2
            tres = sc(6144 + q * 2048, 2048)
            add("dve", STT(TN[q][:, :], X[:, c, b * 512:(b + 1) * 512], sm("final_g", c, 1), rs[:, :],
                           ALU.mult, ALU.mult),
                r=Xr(c, b) + rres + ["SM"], w=tres)
            add("sp", DMA(outT[c * 128:(c + 1) * 128, b * 512:(b + 1) * 512], TN[q][:, :]),
                r=tres, dma=("out", q))

    def final_norm():
        for b in range(NB):
            final_norm_block(b)

    class Stop(Exception):
        pass

    def chk(name):
        if stop == name:
            raise Stop()
    try:
        chk("init")
        while ws.pos < nunits and plan[ws.pos][0] == "mod":
            mod_unit(plan[ws.pos][1], plan[ws.pos][2])
        def make_next_norm(l):
            def nn(b):
                if l + 1 < nlayers:
                    if b == 0:
                        norm_prologue(l + 1, 1)
                    norm_block(l + 1, 1, b)
                else:
                    final_norm_block(b)
            return nn
        for l in range(nlayers):
            if l == 0:
                norm_mod(l, 1)
            dump(f"h1_{l}", H[:, :, :], [("H", c, b) for c in range(KC) for b in range(NB)], [128, KC, S], BF16)
            chk("norm1")
            add("pool", MS(V[:, :, :, 64:65], 1.0), w=[("AR", p) for p in range(32, 49)])
            proj_qk(l)
            chk("qk")
            proj_v(l)
            chk("v")
            proj_f(l)
            dump(f"qt_{l}", QT[:, :, :], [("AR", p) for p in range(16)], [128, 4, S], BF16)
            dump(f"kt_{l}", KT[:, :, :], [("AR", p) for p in range(16, 32)], [128, 4, S], BF16)
            dump(f"v_{l}", V[:, :, :, :], [("AR", p) for p in range(32, 49)], [128, NT, NH, 65], BF16)
            dump(f"nft_{l}", NFT[:, :, :], ["NFT"], [128, NT, NH], F32)
            chk("proj")
            proj_u_pool(l)
            dump(f"pool_{l}", POOLOUT[:, :, :], [("PO", p) for p in range(32)], [128, 4, S], BF16)
            chk("pool")
            wout_pool(l)
            chk("wop")
            attention(l)
            dump(f"ot_{l}", H[:, :, :], [("H", c, b) for c in range(KC) for b in range(NB)], [128, KC, S], BF16)
            chk("attn")
            norm_prologue(l, 2)
            wout_attn(l, after_block=(lambda b, l=l: norm_block(l, 2, b)))
            chk("mix")
            ffn(l, make_next_norm(l))
            ffn_close()
            dump(f"xffn_{l}", X[:, :, :], [("X", c, b) for c in range(KC) for b in range(NB)], [128, KC, S], F32)
        chk("ffn")
        assert ws.pos == nunits, (ws.pos, nunits)
    except Stop:
        add("sp", DMA(outT[0:128, 0:512], X[:, 0, 0:512]), r=[("X", 0, 0)], dma=("out", 0))

    SC.emit()
    dmasem, dmacnt = SC.final
    for k, sem in dmasem.items():
        if isinstance(k, tuple) and k[0] == "out" or (isinstance(k, str) and k.startswith("dbg_")):
            nc.sync.wait_ge(sem, dmacnt[k])
    info = {"WTOT": WTOT, "NS": NS, "sbuf_used": sbuf_used, "stats": SC.stats, "dbg": list(dbg_out)}
    return nc, info


def pack_inputs(inputs, nlayers=L):
    w = {k: np.asarray(v, dtype=np.float32) for k, v in inputs.items()}
    plan = unit_plan(nlayers)
    Wall = np.concatenate([unit_array(k, l, u, w) for (k, l, u) in plan], axis=1)
    Wall = np.ascontiguousarray(Wall)
    slay, NS = smalls_layout(nlayers)
    consts = build_consts()
    in_maps = []

    def fm(vec):
        return vec.reshape(KC, 128).T
    for b in range(8):
        sm = np.zeros((128, NS), np.float32)

        def put(name, arr):
            o, n = slay[name]
            sm[:, o:o + n] = arr
        put("c", fm(w["c"][b]))
        put("final_g", fm(w["final_g"]))
        for l in range(nlayers):
            put(f"modb{l}", w["mod_b"][l].reshape(48, 128).T)
            put(f"n1g{l}", fm(w["norm1_g"][l]))
            put(f"n2g{l}", fm(w["norm2_g"][l]))
            put(f"pscale{l}", w["pool_scale"][l].reshape(4, 128).T)
            cw = w["ffn_conv_w"][l].reshape(3, 44, 128).transpose(2, 0, 1).reshape(128, 132)
            put(f"cw{l}", cw)
            put(f"cb{l}", w["ffn_conv_b"][l].reshape(44, 128).T)
            put(f"bf{l}", np.broadcast_to(np.tile(w["b_f"][l], NT)[None, :], (128, 128)))
        in_maps.append({
            "xT": np.ascontiguousarray(w["x"][b].T),
            "W": Wall,
            "smalls": sm,
            "consts": consts,
        })
    return in_maps


_CACHE = {}


def kernel(**inputs):
    from contextlib import ExitStack
    in_maps = pack_inputs(inputs)
    with ExitStack() as stack:
        nc, info = build_program(L, None, stack)
        res = run_bass_kernel_spmd(nc, in_maps, core_ids=list(range(8)))
    outs = [np.asarray(r["outT"]).T for r in res.results]
    return np.ascontiguousarray(np.stack(outs, axis=0).astype(np.float32))
```

```python
import numpy as np
import concourse.bass as bass
import concourse.mybir as mybir
from concourse.bass_utils import run_bass_kernel_spmd

F32 = mybir.dt.float32
BF16 = mybir.dt.bfloat16
U8 = mybir.dt.uint8
AF = mybir.ActivationFunctionType
ALU = mybir.AluOpType

D = 1024
S = 2048
L = 2
NH = 8
DH = 64
DFF = 2816
KC = 8
NB = 4
NT = 16
INCOLS = 2056
EPS = 1e-6
NPAIR = 11
WINDOWS = (2, 4, 8, 16)
MASKNEG = -30000.0
USZ = 2048
import os
FFNDBG = os.environ.get('FFNDBG', '')


def unit_plan(nlayers=L):
    plan = []
    for u in range(8):
        plan.append(("mod", 0, u))
    for l in range(nlayers):
        nm0 = [8]

        def extra(k):
            if l == 0:
                for _ in range(k):
                    if nm0[0] < 24:
                        plan.append(("mod", 0, nm0[0]))
                        nm0[0] += 1
        for u in range(4):
            plan.append(("qk", l, u))
            extra(2)
        for u in range(2):
            plan.append(("v", l, u))
            extra(2)
        plan.append(("f", l, 0))
        extra(2)
        for u in range(2):
            plan.append(("u", l, u))
            extra(2)
        assert l > 0 or nm0[0] == 24
        plan.append(("poolw", l, 0))
        for u in range(4):
            plan.append(("wop", l, u))
        for u in range(4):
            plan.append(("woa", l, u))
        nmod = 0
        for hh in range(2):
            for p in range(NPAIR):
                plan.append(("up", l, hh * NPAIR + p))
                if l + 1 < nlayers and nmod < 24:
                    plan.append(("mod", l + 1, nmod))
                    nmod += 1
            for dc in range(8):
                plan.append(("down", l, hh * 8 + dc))
                if l + 1 < nlayers and nmod < 24:
                    plan.append(("mod", l + 1, nmod))
                    nmod += 1
        assert l + 1 >= nlayers or nmod == 24
    return plan


def unit_size(kind):
    return {"mod": 2048, "qk": 2048, "v": 2048, "f": 64, "u": 2048, "poolw": 512,
            "wop": 1024, "woa": 2048, "up": 2048, "down": NPAIR * 128}[kind]


def unit_array(kind, l, u, w):
    def kcp(mat):
        return mat.reshape(KC, 128, mat.shape[1]).transpose(1, 0, 2)
    if kind == "mod":
        a = kcp(w["mod_w"][l][:, u * 256:(u + 1) * 256])
    elif kind == "qk":
        a = kcp(w["w_in"][l][:, u * 256:(u + 1) * 256])
    elif kind == "v":
        a = kcp(w["w_in"][l][:, 1024 + u * 256:1024 + (u + 1) * 256])
    elif kind == "f":
        a = kcp(w["w_in"][l][:, 1536:1544])
    elif kind == "u":
        a = kcp(w["w_in"][l][:, 1544 + u * 256:1544 + (u + 1) * 256])
    elif kind == "poolw":
        a = w["pool_w"][l].transpose(1, 0, 2)
    elif kind == "wop":
        m = w["w_out"][l][512:1024, u * 256:(u + 1) * 256]
        a = m.reshape(4, 128, 256).transpose(1, 0, 2)
    elif kind == "woa":
        m = w["w_out"][l][0:512, u * 256:(u + 1) * 256]
        a = np.zeros((128, 8, 256), np.float32)
        a[0:64] = m.reshape(8, 64, 256).transpose(1, 0, 2)
    elif kind == "up":
        hh, p = divmod(u, NPAIR)
        cg = hh * NPAIR + p
        cv = 22 + cg
        m = np.concatenate([w["ffn_up"][l][:, cg * 128:(cg + 1) * 128],
                            w["ffn_up"][l][:, cv * 128:(cv + 1) * 128]], axis=1)
        a = kcp(m)
    elif kind == "down":
        hh, dc = divmod(u, 8)
        m = w["ffn_down"][l][hh * NPAIR * 128:(hh + 1) * NPAIR * 128, dc * 128:(dc + 1) * 128]
        a = m.reshape(NPAIR, 128, 128).transpose(1, 0, 2)
    else:
        raise ValueError(kind)
    return np.ascontiguousarray(a, dtype=np.float32).reshape(128, -1)


def smalls_layout(nlayers=L):
    lay = {}
    off = 0

    def put(name, n):
        nonlocal off
        lay[name] = (off, n)
        off += n
    put("c", 8)
    put("final_g", 8)
    for l in range(nlayers):
        put(f"modb{l}", 48)
        put(f"n1g{l}", 8)
        put(f"n2g{l}", 8)
        put(f"pscale{l}", 4)
        put(f"cw{l}", 3 * 44)
        put(f"cb{l}", 44)
        put(f"bf{l}", 128)
    return lay, off


C_IDENT = 0
C_TRI = 128
C_MASK = 256
C_BAND = 384
NCST = 384 + 12 * 128


def build_consts():
    c = np.zeros((128, NCST), np.float32)
    idx = np.arange(128)
    c[:, C_IDENT:C_IDENT + 128] = np.eye(128, dtype=np.float32)
    tri = (idx[None, :] >= idx[:, None]).astype(np.float32)
    c[:, C_TRI:C_TRI + 128] = tri
    c[:, C_MASK:C_MASK + 128] = (1.0 - tri) * MASKNEG
    for g, w in enumerate(WINDOWS):
        diag = np.zeros((128, 128), np.float32)
        off = np.zeros((128, 128), np.float32)
        diag0 = np.zeros((128, 128), np.float32)
        for t in range(128):
            for k in range(w):
                tp = t - k
                if tp >= 0:
                    diag[tp, t] += 1.0 / w
                else:
                    off[128 + tp, t] += 1.0 / w
            diag[t, t] -= 1.0
            cnt = min(t + 1, w)
            for k in range(cnt):
                diag0[t - k, t] += 1.0 / cnt
            diag0[t, t] -= 1.0
        b = C_BAND + g * 384
        c[:, b:b + 128] = diag
        c[:, b + 128:b + 256] = off
        c[:, b + 256:b + 384] = diag0
    return c


class Op:
    __slots__ = ("eng", "fn", "deps", "sig", "sem", "val", "dma")

    def __init__(self, eng, fn, dma):
        self.eng = eng
        self.fn = fn
        self.dma = dma
        self.deps = ()
        self.sig = False
        self.sem = None
        self.val = 0


class Sched:
    ENGS = ("pe", "act", "dve", "pool", "sp")

    def __init__(self, nc, stack):
        self.nc = nc
        self.stack = stack
        self.ops = []
        self.last_w = {}
        self.readers = {}
        self.handles = {"pe": nc.tensor, "act": nc.scalar, "dve": nc.vector,
                        "pool": nc.gpsimd, "sp": nc.sync}
        self.dma_slots = {}

    def add(self, eng, fn, r=(), w=(), dma=None):
        o = Op(eng, fn, dma)
        deps = set()
        psr_ = [x for x in r if isinstance(x, tuple) and x[0] == "PS"]
        if psr_:
            r = [x for x in r if not (isinstance(x, tuple) and x[0] == "PS")]
            w = list(w) + psr_
        for x in r:
            p = self.last_w.get(x)
            if p is not None:
                deps.add(p)
        for x in w:
            p = self.last_w.get(x)
            if p is not None:
                deps.add(p)
            rd = self.readers.get(x)
            if rd:
                deps.update(rd.values())
        for x in r:
            d = self.readers.setdefault(x, {})
            key = eng if eng != "sp" else ("sp", len(self.ops))
            d[key] = o
        for x in w:
            self.last_w[x] = o
            self.readers[x] = {}
        o.deps = [d for d in deps if not (d.eng == "pe" and eng == "pe")]
        self.ops.append(o)
        return o

    def emit(self):
        nc = self.nc
        for o in self.ops:
            for d in o.deps:
                d.sig = True
        engsem = {}
        for e in self.ENGS:
            engsem[e] = self.stack.enter_context(nc.semaphore("sem_" + e))
        dmasem = {}
        dmacnt = {}
        cnt = {e: 0 for e in self.ENGS}
        for o in self.ops:
            if o.dma is not None:
                if o.dma not in dmasem:
                    dmasem[o.dma] = self.stack.enter_context(nc.semaphore("dma_" + str(o.dma)))
                    dmacnt[o.dma] = 0
                dmacnt[o.dma] += 16
                o.sem = dmasem[o.dma]
                o.val = dmacnt[o.dma]
            elif o.sig:
                cnt[o.eng] += 1
                o.sem = engsem[o.eng]
                o.val = cnt[o.eng]
        waited = {e: {} for e in self.ENGS}
        nwait = 0
        for o in self.ops:
            E = self.handles[o.eng]
            need = {}
            for d in o.deps:
                k = id(d.sem)
                if k not in need or need[k][1] < d.val:
                    need[k] = (d.sem, d.val)
            wd = waited[o.eng]
            for k, (sem, val) in need.items():
                if wd.get(k, 0) < val:
                    E.wait_ge(sem, val)
                    wd[k] = val
                    nwait += 1
            ins = o.fn(E)
            if o.dma is not None:
                ins.then_inc(o.sem, 16)
            elif o.sig:
                ins.then_inc(o.sem, 1)
        self.final = (dmasem, dmacnt)
        self.stats = (len(self.ops), nwait, dict(cnt))


def MM(out, lhsT, rhs, start, stop):
    return lambda E: E.matmul(out, lhsT, rhs, start=start, stop=stop)


def ACTF(out, in_, func, bias=None, scale=None):
    def f(E):
        kw = {}
        if bias is not None:
            kw["bias"] = bias
        if scale is not None:
            kw["scale"] = scale
        return E.activation(out, in_, func, **kw)
    return f


def TT(out, a, b, op):
    return lambda E: E.tensor_tensor(out, a, b, op)


def TS(out, a, s1, op0, s2=None, op1=None):
    if op1 is None:
        return lambda E: E.tensor_scalar(out, a, s1, None, op0)
    return lambda E: E.tensor_scalar(out, a, s1, s2, op0, op1)


def STT(out, a, s, b, op0, op1):
    return lambda E: E.scalar_tensor_tensor(out, a, s, b, op0, op1)


def CP(out, a):
    return lambda E: E.tensor_copy(out, a)


def MS(ap, v):
    return lambda E: E.memset(ap, v)


def DMA(out, in_):
    return lambda E: E.dma_start(out=out, in_=in_)


def build_program(nlayers=L, dbg=None, stack=None, stop=None):
    dbg = dbg or set()
    nc = bass.Bass("TRN2", target_bir_lowering=False)
    plan = unit_plan(nlayers)
    offs = []
    tot = 0
    for (k, l, u) in plan:
        offs.append(tot)
        tot += unit_size(k)
    WTOT = tot
    slay, NS = smalls_layout(nlayers)

    xT = nc.dram_tensor("xT", [D, S], F32, kind="ExternalInput").ap()
    Wd = nc.dram_tensor("W", [128, WTOT], F32, kind="ExternalInput").ap()
    smd = nc.dram_tensor("smalls", [128, NS], F32, kind="ExternalInput").ap()
    cstd = nc.dram_tensor("consts", [128, NCST], F32, kind="ExternalInput").ap()
    outT = nc.dram_tensor("outT", [D, S], F32, kind="ExternalOutput").ap()
    dbg_out = {}

    SC = Sched(nc, stack)

    LIMIT = 229344
    base0 = (nc.sbuf_base + 31) // 32 * 32
    cur = [base0]

    def alloc(name, shape, dtype):
        esz = 4 if dtype == F32 else 2
        n = esz
        for s_ in shape[1:]:
            n *= s_
        off = cur[0]
        cur[0] += (n + 31) // 32 * 32
        assert cur[0] <= LIMIT, (name, cur[0], LIMIT)
        return nc.alloc_sbuf_tensor_at(name, list(shape), dtype, offset=off)

    def alloc_at(name, shape, dtype, off):
        return nc.alloc_sbuf_tensor_at(name, list(shape), dtype, offset=off)

    X = alloc("X", [128, KC, S], F32)
    H = alloc("H", [128, KC, S], BF16)
    ar0 = cur[0]
    QT = alloc("QT", [128, 4, S], BF16)
    KT = alloc("KT", [128, 4, S], BF16)
    V = alloc("V", [128, NT, NH, 65], BF16)
    HID = alloc_at("HID", [128, NPAIR, S], BF16, ar0)
    assert NPAIR * S * 2 <= cur[0] - ar0
    po0 = cur[0]
    PO_SIZE = 16640
    cur[0] += PO_SIZE
    POOLOUT = alloc_at("POOLOUT", [128, 4, S], BF16, po0)
    KX = [alloc_at(f"KX{b}", [128, S], BF16, po0 + b * 4096) for b in range(2)]
    G = [alloc_at(f"G{b}", [128, NT, 67], BF16, po0 + 8192 + b * 2560) for b in range(2)]
    NPT = 3
    PT = [alloc_at(f"PT{k}", [128, 512], BF16, po0 + 13312 + k * 1024) for k in range(NPT)]
    NT0 = 3
    T0ALL = alloc_at("T0ALL", [128, 2 * NT0, 512], F32, po0)
    SG = [alloc_at(f"SG{k}", [128, 512], BF16, po0 + 12288 + k * 1024) for k in range(2)]
    HALO = [alloc_at(f"HALO{k}", [128, 4], F32, po0 + 14336 + k * 32) for k in range(4)]
    TMPA = [alloc_at(f"TMPA{k}", [128, 2], F32, po0 + 14464 + k * 32) for k in range(4)]
    HGV = [alloc_at(f"HGV{k}", [128, 2, 2], F32, po0 + 14720 + k * 32) for k in range(2)]
    HBG = [alloc_at(f"HBG{k}", [128, 2, 1], F32, po0 + 14784 + k * 32) for k in range(2)]
    TMPB = [alloc_at(f"TMPB{k}", [128, 2], F32, po0 + 14592 + k * 32) for k in range(4)]

    def po(off, n):
        return [("PO", p) for p in range(off // 512, (off + n - 1) // 512 + 1)]
    sc0 = cur[0]
    SC_SIZE = 13312
    cur[0] += SC_SIZE
    SQ = [alloc_at(f"SQ{k}", [128, 512], BF16, sc0 + k * 1024) for k in range(2)]
    RSTD = [alloc_at(f"RSTD{k}", [128, 512], F32, sc0 + 2048 + k * 2048) for k in range(2)]
    TN = [alloc_at(f"TN{k}", [128, 512], F32, sc0 + 6144 + k * 2048) for k in range(2)]
    UTR = [alloc_at(f"UTR{k}", [128, 256], BF16, sc0 + k * 512) for k in range(3)]
    MTR = [alloc_at(f"MTR{k}", [128, 512], BF16, sc0 + 1536 + k * 1024) for k in range(2)]
    NLt = alloc_at("NL", [128, 128], F32, sc0 + 3584)
    TOTt = alloc_at("TOT", [128, NT, NH], F32, sc0 + 4096)
    OFFt = alloc_at("OFF", [128, NT, NH], F32, sc0 + 4608)
    NEGt = alloc_at("NEG", [128, 128], F32, sc0 + 5120)
    R1t = alloc_at("R1", [128, 128], F32, sc0 + 5632)
    OSB = [alloc_at(f"OSB{k}", [128, 512], F32, sc0 + k * 2048) for k in range(2)]
    RR = [alloc_at(f"RR{k}", [128, 512], F32, sc0 + 4096) for k in range(1)]
    QX = [alloc_at(f"QX{k}", [128, 512], BF16, sc0 + (6144 if k < 2 else 12288 - 2048) + k * 1024) for k in range(3)]
    RHI = [alloc_at(f"RHI{k}", [128, 512], BF16, sc0 + 8192 + k * 2048) for k in range(2)]
    RLO = [alloc_at(f"RLO{k}", [128, 512], BF16, sc0 + 8192 + 1024 + k * 2048) for k in range(2)]

    def sc(off, n):
        return [("SC", p) for p in range(off // 512, (off + n - 1) // 512 + 1)]
    NWB = 6
    WBF = [alloc(f"WBF{k}", [128, USZ], BF16) for k in range(NWB)]
    CSTB = alloc("CSTB", [128, NCST], BF16)
    ONESB = alloc("ONESB", [128, 128], BF16)
    TRIF = alloc("TRIF", [128, 128], F32)
    ONESF = alloc("ONESF", [128, 128], F32)
    SEL = alloc("SEL", [128, 64], F32)
    SM = alloc("SM", [128, NS], F32)
    CACT = alloc("CACT", [128, 8], F32)
    CACTB = alloc("CACTB", [128, 8], BF16)
    SELB = alloc("SELB", [128, 64], BF16)
    MODT = [alloc(f"MODT{l}", [128, 48], F32) for l in range(nlayers)]
    AA = [alloc(f"AA{l}", [128, 16], F32) for l in range(nlayers)]
    NFT = alloc("NFT", [128, NT, NH], F32)
    HIt = alloc("HI", [128, NT, NH], BF16)
    MIDt = alloc("MID", [128, NT, NH], BF16)
    LOt = alloc("LO", [128, NT, NH], BF16)
    sbuf_used = cur[0] - base0
    nc.alloc_sbuf_tensor("slab", [128, cur[0] - nc.sbuf_base], U8)

    PS = nc.alloc_psum_tensor("PS", [128, 8, 512], F32)

    def psr(b):
        return [("PS", b)]

    def sm(name, a=0, n=None):
        o, ln = slay[name]
        if n is None:
            n = ln - a
        return SM[:, o + a:o + a + n]

    IDENTB = CSTB[:, C_IDENT:C_IDENT + 128]
    MASKB = CSTB[:, C_MASK:C_MASK + 128]

    def band(g, k):
        b = C_BAND + g * 384 + k * 128
        return CSTB[:, b:b + 128]

    add = SC.add

    def dump(name, ap, res, shape, dtype):
        if name not in dbg:
            return
        t = nc.dram_tensor("dbg_" + name, list(shape), dtype, kind="ExternalOutput").ap()
        dbg_out[name] = t
        add("sp", DMA(t, ap), r=res, dma="dbg_" + name)

    add("sp", DMA(SM[:, :], smd), w=["SM"], dma="sm")
    add("pool", DMA(CSTB[:, :], cstd), w=["CSTB"], dma="cst")
    add("sp", DMA(TRIF[:, :], cstd[:, C_TRI:C_TRI + 128]), w=["TRIF"], dma="trif")
    add("pool", MS(ONESB[:, :], 1.0), w=["ONESB"])
    add("pool", MS(ONESF[:, :], 1.0), w=["ONESF"])
    add("pool", MS(SEL[:, :], 0.0), w=["SEL"])
    add("pool", MS(SEL[64:65, :], 1.0), w=["SEL"])
    for c in range(KC):
        add("sp", DMA(X[:, c, :], xT[c * 128:(c + 1) * 128, :]),
            w=[("X", c, b) for b in range(NB)], dma=("x", c))
    add("act", ACTF(CACT[:, :], sm("c"), AF.Silu), r=["SM"], w=["CACT"])
    add("dve", CP(CACTB[:, :], CACT[:, :]), r=["CACT"], w=["CACTB"])
    add("pool", MS(SELB[:, :], 0.0), w=["SELB"])
    add("pool", MS(SELB[64:65, :], 1.0), w=["SELB"])

    class WS:
        pass
    ws = WS()
    ws.loaded = 0
    ws.cast = {}
    ws.ncast = 0
    ws.lru = [0, 1, 2]
    ws.ffn = False

    def wbf_res(b):
        return [("WBF", b)] if b < 2 else sc(0, 4096)
    ws.pos = 0
    nunits = len(plan)

    def ws_load(n):
        k, l, u = plan[n]
        sz = unit_size(k)
        slot = n % NWB
        pp = 128
        add("pool", DMA(WBF[slot][0:pp, 0:sz], Wd[0:pp, offs[n]:offs[n] + sz]),
            w=[("WBF", slot)], dma=("wbf", slot))

    def ws_ensure_load(n):
        while ws.loaded <= n and ws.loaded < nunits:
            ws_load(ws.loaded)
            ws.loaded += 1

    def ws_get(kind, l, u):
        n = ws.pos
        assert plan[n] == (kind, l, u), (plan[n], kind, l, u)
        ws_ensure_load(min(n + NWB - 1, nunits - 1))
        b = n % NWB
        return WBF[b], [("WBF", b)]

    def ws_peek(kind, l, u, k):
        n = ws.pos + k
        assert plan[n] == (kind, l, u), (plan[n], kind, l, u)
        ws_ensure_load(min(ws.pos + NWB - 1, nunits - 1))
        b = n % NWB
        return WBF[b], [("WBF", b)]

    def ws_done(drain=True):
        ws.pos += 1
        if drain:
            while ws.pos < nunits and plan[ws.pos][0] == "mod":
                mod_unit(plan[ws.pos][1], plan[ws.pos][2])

    MODBANK = 6

    def mod_unit(l, u):
        wt, wr = ws_get("mod", l, u)
        w3 = wt[:, :].rearrange("p (k n) -> p k n", k=KC)
        for cc in range(2):
            for kc in range(KC):
                add("pe", MM(PS[:, MODBANK, cc:cc + 1], w3[:, kc, cc * 128:(cc + 1) * 128],
                             CACTB[:, kc:kc + 1], kc == 0, kc == KC - 1),
                    r=wr + ["CACTB"], w=psr(MODBANK))
        o_ = slay[f"modb{l}"][0]
        add("dve", TT(MODT[l][:, 2 * u:2 * u + 2], PS[:, MODBANK, 0:2], SM[:, o_ + 2 * u:o_ + 2 * u + 2], ALU.add),
            r=psr(MODBANK) + ["SM"], w=[("MODT", l, u // 4, u % 4)])
        ws_done(drain=False)

    def MODr(l, part):
        return [("MODT", l, part, k) for k in range(4)]

    def Xr(c, b):
        return [("X", c, b)]

    def Hr(c, b):
        return [("H", c, b)]

    NORMBANKS = (6, 7)
    nrm = {"n": 0}

    def rmsnorm_stats(b):
        k = nrm["n"] % 2
        nrm["n"] += 1
        bank = NORMBANKS[k]
        for c in range(KC):
            q = c % 2
            add("act", ACTF(SQ[q][:, :], X[:, c, b * 512:(b + 1) * 512], AF.Square),
                r=Xr(c, b), w=sc(q * 1024, 1024))
            add("pe", MM(PS[:, bank, :], ONESB[:, :], SQ[q][:, :], c == 0, c == KC - 1),
                r=sc(q * 1024, 1024) + ["ONESB"], w=psr(bank))
        rres = sc(2048 + k * 2048, 2048)
        add("act", ACTF(RSTD[k][:, :], PS[:, bank, :], AF.Ln, bias=EPS, scale=1.0 / D),
            r=psr(bank), w=rres)
        add("act", ACTF(RSTD[k][:, :], RSTD[k][:, :], AF.Exp, scale=-0.5), r=rres, w=rres)
        return RSTD[k], rres

    def norm_prologue(l, which):
        aoff = 0 if which == 1 else 8
        scp = 1 if which == 1 else 4
        add("dve", STT(AA[l][:, aoff:aoff + 8], MODT[l][:, scp * 8:scp * 8 + 8], 1.0,
                       sm(f"n1g{l}" if which == 1 else f"n2g{l}"), ALU.add, ALU.mult),
            r=MODr(l, scp) + ["SM"], w=[("AA", l, which)])

    def norm_block(l, which, b):
        aoff = 0 if which == 1 else 8
        shp = 0 if which == 1 else 3
        shoff = 0 if which == 1 else 24
        rs, rres = rmsnorm_stats(b)
        for c in range(KC):
            q = c % 2
            tres = sc(6144 + q * 2048, 2048)
            add("dve", TT(TN[q][:, :], X[:, c, b * 512:(b + 1) * 512], rs[:, :], ALU.mult),
                r=Xr(c, b) + rres, w=tres)
            add("act", ACTF(H[:, c, b * 512:(b + 1) * 512], TN[q][:, :], AF.Identity,
                            bias=MODT[l][:, shoff + c:shoff + c + 1], scale=AA[l][:, aoff + c:aoff + c + 1]),
                r=tres + MODr(l, shp) + [("AA", l, which)], w=Hr(c, b))

    def norm_mod(l, which):
        norm_prologue(l, which)
        for b in range(NB):
            norm_block(l, which, b)

    rot = {"a": 0, "b": 0}

    def bankA():
        b = rot["a"] % 4
        rot["a"] += 1
        return b

    def bankB():
        b = 4 + rot["b"] % 2
        rot["b"] += 1
        return b

    def QTr(c, b):
        return [("AR", c * 4 + b)]

    def KTr(c, b):
        return [("AR", 16 + c * 4 + b)]

    def Vr(i):
        o = 32768 + i * 1040
        return [("AR", p) for p in range(o // 1024, (o + 1039) // 1024 + 1)]

    def HIDr(kc, b):
        return [("AR", kc * 4 + b)]

    def proj_qk(l):
        ev = 0
        for u in range(4):
            wt, wr = ws_get("qk", l, u)
            w3 = wt[:, :].rearrange("p (k n) -> p k n", k=KC)
            for cc in range(2):
                ch = (u % 2) * 2 + cc
                for b in range(NB):
                    bank = bankA()
                    for kc in range(KC):
                        add("pe", MM(PS[:, bank, :], w3[:, kc, cc * 128:(cc + 1) * 128],
                                     H[:, kc, b * 512:(b + 1) * 512], kc == 0, kc == KC - 1),
                            r=wr + Hr(kc, b), w=psr(bank))
                    if u < 2:
                        add("act", ACTF(QT[:, ch, b * 512:(b + 1) * 512], PS[:, bank, :], AF.Identity, scale=0.125),
                            r=psr(bank), w=QTr(ch, b))
                    else:
                        add("dve", CP(KT[:, ch, b * 512:(b + 1) * 512], PS[:, bank, :]),
                            r=psr(bank), w=KTr(ch, b))
            ws_done()

    def proj_v(l):
        for u in range(2):
            wt, wr = ws_get("v", l, u)
            w3 = wt[:, :].rearrange("p (k n) -> p k n", k=KC)
            for i in range(NT):
                bank = bankB()
                for kc in range(KC):
                    add("pe", MM(PS[:, bank, 0:256], H[:, kc, i * 128:(i + 1) * 128], w3[:, kc, :],
                                 kc == 0, kc == KC - 1),
                        r=wr + Hr(kc, i // 4), w=psr(bank))
                src = PS[:, bank, 0:256].rearrange("p (h d) -> p h d", h=4)
                dst = V[:, i, u * 4:(u + 1) * 4, 0:64]
                eng = "act" if i % 2 == 0 else "dve"
                if eng == "act":
                    add("act", ACTF(dst, src, AF.Identity), r=psr(bank), w=Vr(i))
                else:
                    add("dve", CP(dst, src), r=psr(bank), w=Vr(i))
            ws_done()

    FBANK = 6

    def proj_f(l):
        wt, wr = ws_get("f", l, 0)
        w3 = wt[:, 0:64].rearrange("p (k n) -> p k n", k=KC)
        for i in range(NT):
            for kc in range(KC):
                add("pe", MM(PS[:, FBANK, i * 8:(i + 1) * 8], H[:, kc, i * 128:(i + 1) * 128], w3[:, kc, :],
                             kc == 0, kc == KC - 1),
                    r=wr + Hr(kc, i // 4), w=psr(FBANK))
        nlr = sc(3584, 512)
        add("dve", TT(NLt[:, :], PS[:, FBANK, 0:128], sm(f"bf{l}"), ALU.add), r=psr(FBANK) + ["SM"], w=nlr)
        add("act", ACTF(NLt[:, :], NLt[:, :], AF.Exp, scale=-1.0), r=nlr, w=nlr)
        add("act", ACTF(NLt[:, :], NLt[:, :], AF.Ln, bias=1.0), r=nlr, w=nlr)
        add("pe", MM(PS[:, FBANK, 128:256], TRIF[:, :], NLt[:, :], True, True), r=nlr + ["TRIF"], w=psr(FBANK))
        add("pe", MM(PS[:, FBANK, 256:384], ONESF[:, :], NLt[:, :], True, True), r=nlr + ["ONESF"], w=psr(FBANK))
        totr = sc(4096, 512)
        offr = sc(4608, 512)
        add("dve", CP(TOTt[:, :, :], PS[:, FBANK, 256:384].rearrange("p (i h) -> p i h", h=NH)),
            r=psr(FBANK), w=totr)
        add("pool", MS(OFFt[:, 0, :], 0.0), w=offr)
        for i in range(1, NT):
            add("dve", TT(OFFt[:, i, :], OFFt[:, i - 1, :], TOTt[:, i - 1, :], ALU.add),
                r=totr + offr, w=offr)
        add("dve", TT(NFT[:, :, :], PS[:, FBANK, 128:256].rearrange("p (i h) -> p i h", h=NH), OFFt[:, :, :], ALU.add),
            r=psr(FBANK) + offr, w=["NFT"])
        negr = sc(5120, 512)
        r1r = sc(5632, 512)
        nft2 = NFT[:, :, :].rearrange("p i h -> p (i h)")
        hi2 = HIt[:, :, :].rearrange("p i h -> p (i h)")
        mid2 = MIDt[:, :, :].rearrange("p i h -> p (i h)")
        lo2 = LOt[:, :, :].rearrange("p i h -> p (i h)")
        add("dve", TS(NEGt[:, :], nft2, -1.0, ALU.mult), r=["NFT"], w=negr)
        add("dve", CP(hi2, NEGt[:, :]), r=negr, w=["HI"])
        add("dve", TT(R1t[:, :], NEGt[:, :], hi2, ALU.subtract), r=negr + ["HI"], w=r1r)
        add("dve", CP(mid2, R1t[:, :]), r=r1r, w=["MID"])
        add("dve", TT(NEGt[:, :], R1t[:, :], mid2, ALU.subtract), r=r1r + ["MID"], w=negr)
        add("dve", CP(lo2, NEGt[:, :]), r=negr, w=["LO"])
        ws_done()

    def POr(g, b):
        return po(g * 4096 + b * 1024, 1024)

    def proj_u_pool(l):
        for u in range(2):
            wt, wr = ws_get("u", l, u)
            w3 = wt[:, :].rearrange("p (k n) -> p k n", k=KC)
            prev = None
            bankM = [None, None]
            for i in range(NT):
                bank = bankB()
                for kc in range(KC):
                    add("pe", MM(PS[:, bank, 0:256], H[:, kc, i * 128:(i + 1) * 128], w3[:, kc, :],
                                 kc == 0, kc == KC - 1),
                        r=wr + Hr(kc, i // 4), w=psr(bank))
                k = i % 3
                utres = sc(k * 512, 512)
                if i % 2 == 0:
                    add("act", ACTF(UTR[k][:, :], PS[:, bank, 0:256], AF.Identity), r=psr(bank), w=utres)
                else:
                    add("dve", CP(UTR[k][:, :], PS[:, bank, 0:256]), r=psr(bank), w=utres)
                for gg in range(2):
                    g = u * 2 + gg
                    if i % 4 == 0:
                        bankM[gg] = bankA()
                    bm = bankM[gg]
                    o_ap = PS[:, bm, (i % 4) * 128:(i % 4 + 1) * 128]
                    if i == 0:
                        add("pe", MM(o_ap, UTR[k][:, gg * 128:(gg + 1) * 128], band(g, 2), True, True),
                            r=utres + ["CSTB"], w=psr(bm))
                    else:
                        pk, pres = prev
                        add("pe", MM(o_ap, UTR[pk][:, gg * 128:(gg + 1) * 128], band(g, 1), True, False),
                            r=pres + ["CSTB"], w=psr(bm))
                        add("pe", MM(o_ap, UTR[k][:, gg * 128:(gg + 1) * 128], band(g, 0), False, True),
                            r=utres + ["CSTB"], w=psr(bm))
                    if i % 4 == 3:
                        b = i // 4
                        if gg == 0:
                            add("act", ACTF(POOLOUT[:, g, b * 512:(b + 1) * 512], PS[:, bm, :], AF.Identity),
                                r=psr(bm), w=POr(g, b))
                        else:
                            add("dve", CP(POOLOUT[:, g, b * 512:(b + 1) * 512], PS[:, bm, :]),
                                r=psr(bm), w=POr(g, b))
                prev = (k, utres)
            ws_done()
        wt, wr = ws_get("poolw", l, 0)
        w3 = wt[:, 0:512].rearrange("p (g d) -> p g d", g=4)
        for g in range(4):
            for b in range(NB):
                bank = bankA()
                add("pe", MM(PS[:, bank, :], w3[:, g, :], POOLOUT[:, g, b * 512:(b + 1) * 512], True, True),
                    r=wr + POr(g, b), w=psr(bank))
                ps_ap = sm(f"pscale{l}", g, 1)
                if (g + b) % 2 == 0:
                    add("act", ACTF(POOLOUT[:, g, b * 512:(b + 1) * 512], PS[:, bank, :], AF.Identity, scale=ps_ap),
                        r=psr(bank) + ["SM"], w=POr(g, b))
                else:
                    add("dve", TS(POOLOUT[:, g, b * 512:(b + 1) * 512], PS[:, bank, :], ps_ap, ALU.mult),
                        r=psr(bank) + ["SM"], w=POr(g, b))
        ws_done()

    def resid_update(l, bank, dchunk, b, gcol):
        add("dve", STT(X[:, dchunk, b * 512:(b + 1) * 512], PS[:, bank, :], MODT[l][:, gcol + dchunk:gcol + dchunk + 1],
                       X[:, dchunk, b * 512:(b + 1) * 512], ALU.mult, ALU.add),
            r=psr(bank) + MODr(l, gcol // 8) + Xr(dchunk, b), w=Xr(dchunk, b))

    def wout_pool(l):
        for u in range(4):
            wt, wr = ws_get("wop", l, u)
            w3 = wt[:, 0:1024].rearrange("p (g n) -> p g n", g=4)
            for cc in range(2):
                dch = u * 2 + cc
                for b in range(NB):
                    bank = bankA()
                    for g in range(4):
                        add("pe", MM(PS[:, bank, :], w3[:, g, cc * 128:(cc + 1) * 128],
                                     POOLOUT[:, g, b * 512:(b + 1) * 512], g == 0, g == 3),
                            r=wr + POr(g, b), w=psr(bank))
                    resid_update(l, bank, dch, b, 16)
            ws_done()

    def wout_attn(l, after_block=None):
        units = [ws_peek("woa", l, u, u) for u in range(4)]
        for b in range(NB):
            for u in range(4):
                wt, wr = units[u]
                w3 = wt[:, :].rearrange("p (h n) -> p h n", h=NH)
                for cc in range(2):
                    dch = u * 2 + cc
                    bank = bankA()
                    for h in range(NH):
                        add("pe", MM(PS[:, bank, :], w3[:, h, cc * 128:(cc + 1) * 128],
                                     H[:, h, b * 512:(b + 1) * 512], h == 0, h == NH - 1),
                            r=wr + Hr(h, b), w=psr(bank))
                    resid_update(l, bank, dch, b, 16)
            if after_block is not None:
                after_block(b)
        for u in range(4):
            ws_done(drain=(u == 3))

    SBANKS = (0, 1, 2)
    ACCB = (3, 4, 6)
    BCBANK = 5
    FQB = (7, 7)

    def KXr(buf):
        return po(buf * 4096, 4096)

    def Gr(buf):
        return po(8192 + buf * 2560, 2144)

    def PTr(k):
        return po(13312 + k * 1024, 1024)

    def QXr(k):
        return sc((6144 if k < 2 else 12288 - 2048) + k * 1024, 1024)

    def attn_init():
        add("dve", MS(G[0][:, :, :], 0.0), w=Gr(0))
        add("dve", MS(KX[0][64:128, :], 0.0), w=KXr(0))
        add("dve", MS(KX[0][64:67, :], 1.0), w=KXr(0))
        for k in range(3):
            add("pool", MS(QX[k][:, :], 0.0), w=QXr(k))
        for k in range(2):
            add("pool", MS(OSB[k][64:128, :], 0.0), w=sc(k * 2048, 2048))
        add("pool", MS(G[1][:, :, :], 0.0), w=Gr(1))
        add("pool", MS(KX[1][0:64, :], 0.0), w=KXr(1))
        add("pool", MS(KX[1][0:3, :], 1.0), w=KXr(1))

    def build_head(h):
        par = h % 2
        ch = h // 2
        p0 = 64 * par
        gc = 64 * (1 - par)
        add("dve", CP(KX[par][p0:p0 + 64, :], KT[p0:p0 + 64, ch, :]),
            r=[("AR", 16 + ch * 4 + b) for b in range(NB)], w=KXr(par))
        add("dve", CP(G[par][:, :, gc], HIt[:, :, h]), r=["HI"], w=Gr(par))
        add("dve", CP(G[par][:, :, gc + 1], MIDt[:, :, h]), r=["MID"], w=Gr(par))
        add("dve", CP(G[par][:, :, gc + 2], LOt[:, :, h]), r=["LO"], w=Gr(par))

    qxs = {"n": 0, "par": [None, None, None]}

    def prep_group(h, j):
        par = h % 2
        ch = h // 2
        p0 = 64 * par
        gc = 64 * (1 - par)
        k = qxs["n"] % 3
        bank = FQB[qxs["n"] % 2]
        qxs["n"] += 1
        M = gc + 3
        for ii in range(4):
            i = j * 4 + ii
            add("pe", MM(PS[0:M, bank, ii * 128:(ii + 1) * 128], G[par][:, i, 0:M], IDENTB, True, True),
                r=Gr(par) + ["CSTB"], w=psr(bank))
        add("dve", CP(QX[k][p0:p0 + 64, :], QT[p0:p0 + 64, ch, j * 512:(j + 1) * 512]), r=QTr(ch, j), w=QXr(k))
        add("dve", CP(QX[k][gc:gc + 3, :], PS[gc:gc + 3, bank, :]), r=psr(bank), w=QXr(k))
        return k

    def attention(l):
        attn_init()
        groups = [(h, j) for h in range(NH) for j in range(NB)]
        steps = []
        for gi, (h, j) in enumerate(groups):
            for i in range(4 * j + 4):
                steps.append((gi, h, j, i))
        st = {"acc": 0, "nrm": 0}
        info = {}
        gq = {}
        qxs["par"] = [None, None, None]

        def issue_S(n):
            gi, h, j, i = steps[n]
            par = h % 2
            sb = SBANKS[n % 3]
            d = i - 4 * j
            c0 = 128 * d if d > 0 else 0
            diag = d >= 0
            qk = gq[gi]
            add("pe", MM(PS[:, sb, c0:512], KX[par][:, i * 128:(i + 1) * 128], QX[qk][:, c0:512], True, not diag),
                r=KXr(par) + QXr(qk), w=psr(sb))
            if diag:
                add("pe", MM(PS[:, sb, c0:c0 + 128], IDENTB, MASKB, False, True), r=["CSTB"], w=psr(sb))
            k = n % NPT
            add("act", ACTF(PT[k][:, c0:512], PS[:, sb, c0:512], AF.Exp, bias=NFT[:, i, h:h + 1]),
                r=psr(sb) + ["NFT"], w=PTr(k))
            info[n] = (k, c0)

        def issue_PV(n):
            gi, h, j, i = steps[n]
            k, c0 = info.pop(n)
            last = (i == 4 * j + 3)
            if i == 0:
                st["accb"] = ACCB[st["acc"] % 3]
                st["acc"] += 1
            ab = st["accb"]
            add("pe", MM(PS[0:65, ab, c0:512], V[:, i, h, 0:65], PT[k][:, c0:512], i == 0, last),
                r=Vr(i) + PTr(k), w=psr(ab))
            if last:
                q = st["nrm"] % 2
                st["nrm"] += 1
                lres = sc(8192 + q * 2048, 2048)
                rres = sc(4096, 2048)
                ores = sc(q * 2048, 2048)
                for p_ in list(pending):
                    if p_[2] == q:
                        pending.remove(p_)
                        p_[1]()
                add("dve", CP(OSB[q][0:65, :], PS[0:65, ab, :]), r=psr(ab), w=ores)

                def recip(q=q, lres=lres, ores=ores, j=j):
                    if j == 3:
                        add("act", ACTF(OSB[q][64:65, :], OSB[q][64:65, :], AF.Ln), r=ores, w=ores)
                        add("act", ACTF(OSB[q][64:65, :], OSB[q][64:65, :], AF.Exp, scale=-1.0), r=ores, w=ores)
                    else:
                        add("dve", lambda E, q=q: E.reciprocal(OSB[q][64:65, :], OSB[q][64:65, :]), r=ores, w=ores)
                pending.append([3, recip, q])

                def tail(q=q, h=h, j=j, lres=lres, ores=ores):
                    add("pe", MM(PS[0:64, BCBANK, :], SEL[:, :], OSB[q][:, :], True, True),
                        r=ores + ["SEL"], w=psr(BCBANK))
                    add("dve", TT(H[0:64, h, j * 512:(j + 1) * 512], OSB[q][0:64, :], PS[0:64, BCBANK, :], ALU.mult),
                        r=ores + psr(BCBANK), w=Hr(h, j))
                pending.append([DEFER, tail, q])

        pending = []
        DEFER = 18

        def tick():
            for p_ in list(pending):
                p_[0] -= 1
                if p_[0] <= 0:
                    pending.remove(p_)
                    p_[1]()

        build_head(0)
        gq[0] = prep_group(*groups[0])
        gq[1] = prep_group(*groups[1])
        N = len(steps)
        LOOK = 2
        for n in range(N + LOOK):
            if n < N:
                gi, h, j, i = steps[n]
                issue_S(n)
                if i == 0:
                    if j == 0 and h + 1 < NH:
                        build_head(h + 1)
                    if gi + 2 < len(groups):
                        gq[gi + 2] = prep_group(*groups[gi + 2])
            if n - LOOK >= 0:
                issue_PV(n - LOOK)
            tick()
        while pending:
            tick()

    def ffn(l, next_norm=None):
        cw = slay[f"cw{l}"][0]
        cb = slay[f"cb{l}"][0]

        def CW(j, c):
            return SM[:, cw + j * 44 + c:cw + j * 44 + c + 1]

        def CB(c):
            return SM[:, cb + c:cb + c + 1]
        t0 = {"n": 0}
        pend = [None]
        uniq = [("HBG", k, g_) for k in range(2) for g_ in range(2)] + [("HB", k) for k in range(4)] + [("HGV", k, g_) for k in range(2) for g_ in range(2)]
        for k in range(4):
            add("pool", MS(HALO[k][:, :], 0.0), w=po(14336, 512) + uniq)
        for hh in range(2):
            for p in range(NPAIR):
                wt, wr = ws_get("up", l, hh * NPAIR + p)
                w3 = wt[:, :].rearrange("p (k n) -> p k n", k=KC)
                cg = hh * NPAIR + p
                cv = 22 + cg
                for b in range(NB):
                    r_ = t0["n"] % 3
                    t0["n"] += 1
                    bg, bv = 2 * r_, 2 * r_ + 1
                    for (bank, cc) in ((bg, 0), (bv, 1)):
                        for kc in range(KC):
                            add("pe", MM(PS[:, bank, :], w3[:, kc, cc * 128:(cc + 1) * 128],
                                         H[:, kc, b * 512:(b + 1) * 512], kc == 0, kc == KC - 1),
                                r=wr + Hr(kc, b), w=psr(bank))
                    outs = []
                    chains = ((bg, cg, r_, r_ * 2048, 0), (bv, cv, NT0 + r_, (NT0 + r_) * 2048, 1))
                    hpar = b % 2
                    for (bank, ch, T0, toff, gv) in chains:
                        tres = po(toff, 2048)
                        add("act", ACTF(T0ALL[:, T0, :], PS[:, bank, :], AF.Identity, bias=CB(ch), scale=CW(2, ch)),
                            r=psr(bank) + ["SM"], w=tres)
                        if b < NB - 1:
                            add("act", ACTF(HGV[hpar][:, gv, 0:2], PS[:, bank, 510:512], AF.Identity, scale=CW(0, ch)),
                                r=psr(bank) + ["SM"], w=[("HGV", hpar, gv)])
                            add("act", ACTF(HBG[hpar][:, gv, 0:1], PS[:, bank, 511:512], AF.Identity, scale=CW(1, ch)),
                                r=psr(bank) + ["SM"], w=[("HBG", hpar, gv)])
                        outs.append((T0, tres))
                    if pend[0] is not None:
                        pend[0]()
                        pend[0] = None
                    for (bank, ch, T0, toff, gv) in chains:
                        tres = po(toff, 2048)
                        add("dve", STT(T0ALL[:, T0, 1:512], PS[:, bank, 0:511], CW(1, ch), T0ALL[:, T0, 1:512], ALU.mult, ALU.add),
                            r=psr(bank) + tres + ["SM"], w=tres)
                    for (bank, ch, T0, toff, gv) in chains:
                        tres = po(toff, 2048)
                        add("dve", STT(T0ALL[:, T0, 2:512], PS[:, bank, 0:510], CW(0, ch), T0ALL[:, T0, 2:512], ALU.mult, ALU.add),
                            r=psr(bank) + tres + ["SM"], w=tres)
                    if b > 0:
                        pp_ = (b + 1) % 2
                        t4 = T0ALL[:, :, :].rearrange("p (g r) n -> p g r n", g=2)
                        both = po(r_ * 2048, 2048) + po((NT0 + r_) * 2048, 2048)
                        add("dve", TT(t4[:, :, r_, 0:2], t4[:, :, r_, 0:2], HGV[pp_][:, :, :], ALU.add),
                            r=[("HGV", pp_, 0), ("HGV", pp_, 1)] + both, w=both)
                        add("dve", TT(t4[:, :, r_, 0:1], t4[:, :, r_, 0:1], HBG[pp_][:, :, :], ALU.add),
                            r=[("HBG", pp_, 0), ("HBG", pp_, 1)] + both, w=both)
                    (Tg, gres), (Tv, vres) = outs
                    q = (t0["n"]) % 2

                    def fin(q=q, Tg=Tg, gres=gres, Tv=Tv, vres=vres, p=p, b=b):
                        sgres = po(12288 + q * 1024, 1024)
                        add("act", ACTF(SG[q][:, :], T0ALL[:, Tg, :], AF.Silu), r=gres, w=sgres)
                        add("pool", TT(HID[:, p, b * 512:(b + 1) * 512], SG[q][:, :], T0ALL[:, Tv, :], ALU.mult),
                            r=sgres + vres, w=HIDr(p, b))
                    pend[0] = fin
                ws_done()
            if pend[0] is not None:
                pend[0]()
                pend[0] = None
            if hh == 0 or next_norm is None:
                for dc in range(8):
                    wt, wr = ws_get("down", l, hh * 8 + dc)
                    w3 = wt[:, 0:NPAIR * 128].rearrange("p (k n) -> p k n", k=NPAIR)
                    for b in range(NB):
                        bank = 6 + (dc * NB + b) % 2
                        for kc in range(NPAIR):
                            add("pe", MM(PS[:, bank, :], w3[:, kc, :], HID[:, kc, b * 512:(b + 1) * 512],
                                         kc == 0, kc == NPAIR - 1),
                                r=wr + HIDr(kc, b), w=psr(bank))
                        resid_update(l, bank, dc, b, 40)
                    ws_done()
            else:
                nb_ = 0
                for grp in range(2):
                    units = [ws_peek("down", l, hh * 8 + grp * 4 + k, k) for k in range(4)]
                    for b in range(NB):
                        for k in range(4):
                            dc = grp * 4 + k
                            wt, wr = units[k]
                            w3 = wt[:, 0:NPAIR * 128].rearrange("p (k n) -> p k n", k=NPAIR)
                            bank = nb_ % 4
                            nb_ += 1
                            for kc in range(NPAIR):
                                add("pe", MM(PS[:, bank, :], w3[:, kc, :], HID[:, kc, b * 512:(b + 1) * 512],
                                             kc == 0, kc == NPAIR - 1),
                                    r=wr + HIDr(kc, b), w=psr(bank))
                            resid_update(l, bank, dc, b, 40)
                        if grp == 1:
                            next_norm(b)
                    for k in range(4):
                        ws_done(drain=(k == 3))

    def ffn_close():
        uniq = [("HBG", k, g_) for k in range(2) for g_ in range(2)] + [("HGV", k, g_) for k in range(2) for g_ in range(2)]
        add("pool", MS(HALO[0][:, :], 0.0), r=uniq, w=po(14336, 512))

    def final_norm_block(b):
        rs, rres = rmsnorm_stats(b)
        for c in range(KC):
            q = c % 2
            tres = sc(6144 + q * 2048, 2048)
            add("dve", STT(TN[q][:, :], X[:, c, b * 512:(b + 1) * 512], sm("final_g", c, 1), rs[:, :],
                           ALU.mult, ALU.mult),
                r=Xr(c, b) + rres + ["SM"], w=tres)
            add("sp", DMA(outT[c * 128:(c + 1) * 128, b * 512:(b + 1) * 512], TN[q][:, :]),
                r=tres, dma=("out", q))

    def final_norm():
        for b in range(NB):
            final_norm_block(b)

    class Stop(Exception):
        pass

    def chk(name):
        if stop == name:
            raise Stop()
    try:
        chk("init")
        while ws.pos < nunits and plan[ws.pos][0] == "mod":
            mod_unit(plan[ws.pos][1], plan[ws.pos][2])
        def make_next_norm(l):
            def nn(b):
                if l + 1 < nlayers:
                    if b == 0:
                        norm_prologue(l + 1, 1)
                    norm_block(l + 1, 1, b)
                else:
                    final_norm_block(b)
            return nn
        for l in range(nlayers):
            if l == 0:
                norm_mod(l, 1)
            dump(f"h1_{l}", H[:, :, :], [("H", c, b) for c in range(KC) for b in range(NB)], [128, KC, S], BF16)
            chk("norm1")
            add("pool", MS(V[:, :, :, 64:65], 1.0), w=[("AR", p) for p in range(32, 49)])
            proj_qk(l)
            chk("qk")
            proj_v(l)
            chk("v")
            proj_f(l)
            dump(f"qt_{l}", QT[:, :, :], [("AR", p) for p in range(16)], [128, 4, S], BF16)
            dump(f"kt_{l}", KT[:, :, :], [("AR", p) for p in range(16, 32)], [128, 4, S], BF16)
            dump(f"v_{l}", V[:, :, :, :], [("AR", p) for p in range(32, 49)], [128, NT, NH, 65], BF16)
            dump(f"nft_{l}", NFT[:, :, :], ["NFT"], [128, NT, NH], F32)
            chk("proj")
            proj_u_pool(l)
            dump(f"pool_{l}", POOLOUT[:, :, :], [("PO", p) for p in range(32)], [128, 4, S], BF16)
            chk("pool")
            wout_pool(l)
            chk("wop")
            attention(l)
            dump(f"ot_{l}", H[:, :, :], [("H", c, b) for c in range(KC) for b in range(NB)], [128, KC, S], BF16)
            chk("attn")
            norm_prologue(l, 2)
            wout_attn(l, after_block=(lambda b, l=l: norm_block(l, 2, b)))
            chk("mix")
            ffn(l, make_next_norm(l))
            ffn_close()
            dump(f"xffn_{l}", X[:, :, :], [("X", c, b) for c in range(KC) for b in range(NB)], [128, KC, S], F32)
        chk("ffn")
        assert ws.pos == nunits, (ws.pos, nunits)
    except Stop:
        add("sp", DMA(outT[0:128, 0:512], X[:, 0, 0:512]), r=[("X", 0, 0)], dma=("out", 0))

    SC.emit()
    dmasem, dmacnt = SC.final
    for k, sem in dmasem.items():
        if isinstance(k, tuple) and k[0] == "out" or (isinstance(k, str) and k.startswith("dbg_")):
            nc.sync.wait_ge(sem, dmacnt[k])
    info = {"WTOT": WTOT, "NS": NS, "sbuf_used": sbuf_used, "stats": SC.stats, "dbg": list(dbg_out)}
    return nc, info


def pack_inputs(inputs, nlayers=L):
    w = {k: np.asarray(v, dtype=np.float32) for k, v in inputs.items()}
    plan = unit_plan(nlayers)
    Wall = np.concatenate([unit_array(k, l, u, w) for (k, l, u) in plan], axis=1)
    Wall = np.ascontiguousarray(Wall)
    slay, NS = smalls_layout(nlayers)
    consts = build_consts()
    in_maps = []

    def fm(vec):
        return vec.reshape(KC, 128).T
    for b in range(8):
        sm = np.zeros((128, NS), np.float32)

        def put(name, arr):
            o, n = slay[name]
            sm[:, o:o + n] = arr
        put("c", fm(w["c"][b]))
        put("final_g", fm(w["final_g"]))
        for l in range(nlayers):
            put(f"modb{l}", w["mod_b"][l].reshape(48, 128).T)
            put(f"n1g{l}", fm(w["norm1_g"][l]))
            put(f"n2g{l}", fm(w["norm2_g"][l]))
            put(f"pscale{l}", w["pool_scale"][l].reshape(4, 128).T)
            cw = w["ffn_conv_w"][l].reshape(3, 44, 128).transpose(2, 0, 1).reshape(128, 132)
            put(f"cw{l}", cw)
            put(f"cb{l}", w["ffn_conv_b"][l].reshape(44, 128).T)
            put(f"bf{l}", np.broadcast_to(np.tile(w["b_f"][l], NT)[None, :], (128, 128)))
        in_maps.append({
            "xT": np.ascontiguousarray(w["x"][b].T),
            "W": Wall,
            "smalls": sm,
            "consts": consts,
        })
    return in_maps


_CACHE = {}


def kernel(**inputs):
    from contextlib import ExitStack
    in_maps = pack_inputs(inputs)
    with ExitStack() as stack:
        nc, info = build_program(L, None, stack)
        res = run_bass_kernel_spmd(nc, in_maps, core_ids=list(range(8)))
    outs = [np.asarray(r["outT"]).T for r in res.results]
    return np.ascontiguousarray(np.stack(outs, axis=0).astype(np.float32))
```
